# Optimizing a Trainium2 kernel written in Bass

```python
import math
import jax
import jax.numpy as jnp
from jax import lax
import numpy as np

D_MODEL = 1024
BATCH = 8
SEQ = 8192
DEPTH = 2

GRID_W = 64
CTX_LEN = 256
N_EVEN = (DEPTH + 1) // 2
N_ODD = DEPTH // 2
EPS = 1e-6

MIX_W = D_MODEL
HY_W = MIX_W // 2
HY_ORDER = 2
HY_SHORT = 3
HY_BANDS = 16
HY_EMB = 1 + 2 * HY_BANDS
HY_FILTER_HID = 64
HY_MOD_SHIFT = 0.05
HY_FAST_DECAY = 0.3
HY_SLOW_DECAY = 1.5
HY_DECAY_TARGET = 1e-2
LRU_W = MIX_W - HY_W
LRU_HEADS = 8
LRU_BW = LRU_W // LRU_HEADS
LRU_CONV = 4
LRU_C = 8.0
IN_W = (HY_ORDER + 1) * HY_W + 2 * LRU_W

HEAD_DIM = 64
N_HEADS = D_MODEL // HEAD_DIM
N_KV = 4
GROUP = N_HEADS // N_KV
WINDOW = 128
BLOCK = 128
ROPE_BASE = 10000.0
QKV_W = (N_HEADS + 2 * N_KV) * HEAD_DIM
NEG_INF = -1e30

D_FF = 256 * (-(-(8 * D_MODEL) // (3 * 256)))

kernel_name = 'hybrid_hyena_rglru_swa_prefix_block'


def rms_norm(x, g):
    xf = x.astype(jnp.float32)
    y = xf * lax.rsqrt(jnp.mean(xf * xf, axis=-1, keepdims=True) + EPS)
    return (y * g.astype(jnp.float32)).astype(x.dtype)


def dw_conv(x, w, b):
    k = w.shape[0]
    pl, pr = k // 2, k - 1 - k // 2
    y = lax.conv_general_dilated(x, w[:, None, :], window_strides=(1,), padding=[(pl, pr)],
                                 dimension_numbers=('NWC', 'WIO', 'NWC'),
                                 feature_group_count=x.shape[-1])
    return y + b


def swiglu(h, w1, w3, w2):
    return (jax.nn.silu(h @ w1) * (h @ w3)) @ w2


def hyena_filters(L, w1, b1, w2, b2, w3, freq):
    f32 = jnp.float32
    t = jnp.linspace(0.0, 1.0, L, dtype=f32)[:, None]
    bands = jnp.linspace(1e-4, HY_BANDS - 1, HY_BANDS, dtype=f32)
    w = 2.0 * math.pi * jnp.arange(L, dtype=f32)[:, None] / L
    z = jnp.concatenate([t, jnp.cos(bands * w), -jnp.sin(bands * w)], axis=-1)
    fr = freq.astype(f32)
    h = jnp.sin(fr * (z @ w1.astype(f32) + b1.astype(f32)))
    h = jnp.sin(fr * (h @ w2.astype(f32) + b2.astype(f32)))
    h = (h @ w3.astype(f32)).reshape(L, 2, HY_ORDER, HY_W)
    max_decay = math.log(HY_DECAY_TARGET) / HY_FAST_DECAY
    min_decay = math.log(HY_DECAY_TARGET) / HY_SLOW_DECAY
    deltas = jnp.abs(jnp.linspace(min_decay, max_decay, HY_W, dtype=f32))
    window = jnp.exp(-t * deltas) + HY_MOD_SHIFT
    h = h * window[:, None, None, :]
    fwd = h[:, 0]
    bwd = h[1:, 1]
    k = jnp.concatenate([fwd, jnp.zeros((1, HY_ORDER, HY_W), f32), bwd[::-1]], axis=0)
    return k / jnp.sum(jnp.abs(k), axis=0, keepdims=True)


def long_conv(u, kf):
    L = u.shape[1]
    U = jnp.fft.rfft(u, n=2 * L, axis=1)
    return jnp.fft.irfft(U * kf, n=2 * L, axis=1)[:, :L]


def hyena(u, conv_w, conv_b, filt, bias):
    L = u.shape[1]
    uc = dw_conv(u, conv_w, conv_b).astype(jnp.float32)
    v, x1, x2 = jnp.split(uc, HY_ORDER + 1, axis=-1)
    kf = jnp.fft.rfft(hyena_filters(L, *filt), axis=0)
    bias = bias.astype(jnp.float32)
    z = v
    for n, gate in enumerate((x1, x2)):
        z = gate * (long_conv(z, kf[:, n]) + bias[n] * z)
    return z.astype(u.dtype)


def rglru_coeffs(x, w_a, b_a, w_i, b_i, lam):
    xh = x.reshape(x.shape[:-1] + (LRU_HEADS, LRU_BW))
    r = jax.nn.sigmoid(jnp.einsum('blhi,hij->blhj', xh, w_a).reshape(x.shape) + b_a)
    i = jax.nn.sigmoid(jnp.einsum('blhi,hij->blhj', xh, w_i).reshape(x.shape) + b_i)
    log_a = -LRU_C * r * jax.nn.softplus(-lam)
    a = jnp.exp(log_a)
    b = jnp.sqrt(-jnp.expm1(2.0 * log_a)) * (i * x)
    return a, b


def linear_scan(a, b, h0, reverse):
    if h0 is not None:
        idx = -1 if reverse else 0
        b = b.at[:, idx].add(a[:, idx] * h0)

    def combine(e1, e2):
        a1, b1 = e1
        a2, b2 = e2
        return a1 * a2, a2 * b1 + b2

    _, h = lax.associative_scan(combine, (a, b), reverse=reverse, axis=1)
    return h


def hyena_lru_mixer(hx, hc, w_in, hy_conv_w, hy_conv_b, filt, hy_bias,
                    lru_conv_w, lru_conv_b, lru_params, w_out, ctx_out):
    f32 = jnp.float32
    s_hy = (HY_ORDER + 1) * HY_W
    s_lru = s_hy + LRU_W
    px = hx @ w_in
    hy_x, lx, gx = px[..., :s_hy], px[..., s_hy:s_lru], px[..., s_lru:]
    if ctx_out:
        pc = hc @ w_in
        hy_c, lc, gc = pc[..., :s_hy], pc[..., s_hy:s_lru], pc[..., s_lru:]
    else:
        lc = hc @ w_in[:, s_hy:s_lru]
    lx = dw_conv(lx, lru_conv_w, lru_conv_b).astype(f32)
    lc = dw_conv(lc, lru_conv_w, lru_conv_b).astype(f32)
    rx = 0.0
    rc = 0.0
    for d, reverse in enumerate((False, True)):
        prm = tuple(p[d].astype(f32) for p in lru_params)
        a, b = rglru_coeffs(lc, *prm)
        hcs = linear_scan(a, b, None, reverse)
        h_end = hcs[:, 0] if reverse else hcs[:, -1]
        a, b = rglru_coeffs(lx, *prm)
        rx = rx + linear_scan(a, b, h_end, reverse)
        if ctx_out:
            rc = rc + hcs
    rec_x = (rx * jax.nn.gelu(gx.astype(f32))).astype(hx.dtype)
    out_x = jnp.concatenate([hyena(hy_x, hy_conv_w, hy_conv_b, filt, hy_bias), rec_x], axis=-1) @ w_out
    if not ctx_out:
        return out_x, None
    rec_c = (rc * jax.nn.gelu(gc.astype(f32))).astype(hc.dtype)
    out_c = jnp.concatenate([hyena(hy_c, hy_conv_w, hy_conv_b, filt, hy_bias), rec_c], axis=-1) @ w_out
    return out_x, out_c


def rope_1d(x, pos):
    nf = x.shape[-1] // 2
    inv = jnp.power(ROPE_BASE, -jnp.arange(nf, dtype=jnp.float32) / nf)
    ang = pos.astype(jnp.float32)[:, None] * inv
    cos = jnp.cos(ang)[None, :, None, :]
    sin = jnp.sin(ang)[None, :, None, :]
    xf = x.astype(jnp.float32)
    x1, x2 = xf[..., :nf], xf[..., nf:]
    return jnp.concatenate([x1 * cos - x2 * sin, x2 * cos + x1 * sin], axis=-1).astype(x.dtype)


def rope_2d(x, row, col):
    half = HEAD_DIM // 2
    return jnp.concatenate([rope_1d(x[..., :half], row), rope_1d(x[..., half:], col)], axis=-1)


def window_attention(hx, hc, w_qkv, q_gain, k_gain, sink, w_o, ctx_out):
    f32 = jnp.float32
    B_, S, _ = hx.shape
    C = hc.shape[1]
    nq = N_HEADS * HEAD_DIM
    nkv = N_KV * HEAD_DIM
    scale = HEAD_DIM ** -0.5

    def heads(t, n):
        return t.reshape(t.shape[:-1] + (n, HEAD_DIM))

    qkv_l = hx @ w_qkv
    ql = rms_norm(heads(qkv_l[..., :nq], N_HEADS), q_gain)
    kl = rms_norm(heads(qkv_l[..., nq:nq + nkv], N_KV), k_gain)
    vl = heads(qkv_l[..., nq + nkv:], N_KV)
    if ctx_out:
        qkv_c = hc @ w_qkv
        qc = rms_norm(heads(qkv_c[..., :nq], N_HEADS), q_gain)
        kvc = qkv_c[..., nq:]
    else:
        kvc = hc @ w_qkv[:, nq:]
    kc = rms_norm(heads(kvc[..., :nkv], N_KV), k_gain)
    vc = heads(kvc[..., nkv:], N_KV)

    rows = S // GRID_W
    row = jnp.repeat(jnp.arange(rows, dtype=jnp.int32), GRID_W)
    col = jnp.tile(jnp.arange(GRID_W, dtype=jnp.int32), rows)
    ql = rope_2d(ql, row, col)
    kl = rope_2d(kl, row, col)

    sink_b = sink.astype(f32).reshape(N_KV, GROUP)[None, :, :, None, None]
    qg = ql.reshape(B_, S, N_KV, GROUP, HEAD_DIM)
    kp = jnp.pad(kl, ((0, 0), (BLOCK, BLOCK), (0, 0), (0, 0)))
    vp = jnp.pad(vl, ((0, 0), (BLOCK, BLOCK), (0, 0), (0, 0)))

    def block(n):
        start = n * BLOCK
        qb = lax.dynamic_slice_in_dim(qg, start, BLOCK, axis=1)
        kb = lax.dynamic_slice_in_dim(kp, start, 3 * BLOCK, axis=1)
        vb = lax.dynamic_slice_in_dim(vp, start, 3 * BLOCK, axis=1)
        s_loc = jnp.einsum('bqkgd,bskd->bkgqs', qb, kb).astype(f32) * scale
        s_ctx = jnp.einsum('bqkgd,bckd->bkgqc', qb, kc).astype(f32) * scale
        qpos = start + jnp.arange(BLOCK)
        kpos = start - BLOCK + jnp.arange(3 * BLOCK)
        valid = (jnp.abs(qpos[:, None] - kpos[None, :]) <= WINDOW) & (kpos >= 0) & (kpos < S)
        s_loc = jnp.where(valid, s_loc, NEG_INF)
        s_sink = jnp.broadcast_to(sink_b, s_ctx.shape[:-1] + (1,))
        p = jax.nn.softmax(jnp.concatenate([s_sink, s_ctx, s_loc], axis=-1), axis=-1)
        p_ctx = p[..., 1:1 + C].astype(vc.dtype)
        p_loc = p[..., 1 + C:].astype(vb.dtype)
        return (jnp.einsum('bkgqc,bckd->bqkgd', p_ctx, vc)
                + jnp.einsum('bkgqs,bskd->bqkgd', p_loc, vb))

    o = lax.map(block, jnp.arange(S // BLOCK))
    out_x = jnp.moveaxis(o, 0, 1).reshape(B_, S, nq) @ w_o
    if not ctx_out:
        return out_x, None
    qcg = qc.reshape(B_, C, N_KV, GROUP, HEAD_DIM)
    s = jnp.einsum('bqkgd,bckd->bkgqc', qcg, kc).astype(f32) * scale
    s = jnp.concatenate([jnp.broadcast_to(sink_b, s.shape[:-1] + (1,)), s], axis=-1)
    p = jax.nn.softmax(s, axis=-1)[..., 1:].astype(vc.dtype)
    out_c = jnp.einsum('bkgqc,bckd->bqkgd', p, vc).reshape(B_, C, nq) @ w_o
    return out_x, out_c


def setup_inputs(seed: int = 0) -> dict:
    key = jax.random.key(seed)
    ks = iter(jax.random.split(key, 64))
    f32 = jnp.float32
    D = D_MODEL

    def nrm(shape, scale):
        return jax.random.normal(next(ks), shape, f32) * scale

    def gain(shape):
        return 1.0 + nrm(shape, 0.02)

    a_c = jax.random.uniform(next(ks), (N_EVEN, 2, LRU_W), f32, 0.9, 0.999)
    a = a_c ** (1.0 / LRU_C)
    lam = jnp.log(a) - jnp.log1p(-a)

    return {
        'x': nrm((BATCH, SEQ, D), 1.0),
        'c': nrm((BATCH, D), 1.0),
        'ctx': nrm((BATCH, CTX_LEN, D), 1.0),
        'c_ctx': nrm((D,), 1.0),
        'norm1': gain((DEPTH, D)),
        'norm2': gain((DEPTH, D)),
        'w_mod': nrm((DEPTH, D, 6 * D), 0.5 * D ** -0.5),
        'b_mod': nrm((DEPTH, 6 * D), 0.02),
        'ffn_w1': nrm((DEPTH, D, D_FF), D ** -0.5),
        'ffn_w3': nrm((DEPTH, D, D_FF), D ** -0.5),
        'ffn_w2': nrm((DEPTH, D_FF, D), D_FF ** -0.5),
        'ab_w_in': nrm((N_EVEN, D, IN_W), D ** -0.5),
        'hy_conv_w': nrm((N_EVEN, HY_SHORT, (HY_ORDER + 1) * HY_W), 0.5),
        'hy_conv_b': nrm((N_EVEN, (HY_ORDER + 1) * HY_W), 0.02),
        'hy_f_w1': nrm((N_EVEN, HY_EMB, HY_FILTER_HID), HY_EMB ** -0.5),
        'hy_f_b1': nrm((N_EVEN, HY_FILTER_HID), 0.02),
        'hy_f_w2': nrm((N_EVEN, HY_FILTER_HID, HY_FILTER_HID), HY_FILTER_HID ** -0.5),
        'hy_f_b2': nrm((N_EVEN, HY_FILTER_HID), 0.02),
        'hy_f_w3': nrm((N_EVEN, HY_FILTER_HID, 2 * HY_ORDER * HY_W), HY_FILTER_HID ** -0.5),
        'hy_f_freq': gain((N_EVEN, HY_FILTER_HID)),
        'hy_bias': nrm((N_EVEN, HY_ORDER, HY_W), 0.5),
        'lru_conv_w': nrm((N_EVEN, LRU_CONV, LRU_W), 0.5),
        'lru_conv_b': nrm((N_EVEN, LRU_W), 0.02),
        'lru_w_a': nrm((N_EVEN, 2, LRU_HEADS, LRU_BW, LRU_BW), LRU_BW ** -0.5),
        'lru_b_a': nrm((N_EVEN, 2, LRU_W), 0.02),
        'lru_w_i': nrm((N_EVEN, 2, LRU_HEADS, LRU_BW, LRU_BW), LRU_BW ** -0.5),
        'lru_b_i': nrm((N_EVEN, 2, LRU_W), 0.02),
        'lru_lam': lam,
        'ab_w_out': nrm((N_EVEN, MIX_W, D), MIX_W ** -0.5),
        'at_w_qkv': nrm((N_ODD, D, QKV_W), D ** -0.5),
        'at_q_gain': gain((N_ODD, HEAD_DIM)),
        'at_k_gain': gain((N_ODD, HEAD_DIM)),
        'at_sink': nrm((N_ODD, N_HEADS), 0.5),
        'at_w_o': nrm((N_ODD, N_HEADS * HEAD_DIM, D), (N_HEADS * HEAD_DIM) ** -0.5),
    }


def reference(x, c, ctx, c_ctx, norm1, norm2, w_mod, b_mod, ffn_w1, ffn_w3, ffn_w2,
              ab_w_in, hy_conv_w, hy_conv_b, hy_f_w1, hy_f_b1, hy_f_w2, hy_f_b2, hy_f_w3,
              hy_f_freq, hy_bias, lru_conv_w, lru_conv_b, lru_w_a, lru_b_a, lru_w_i, lru_b_i,
              lru_lam, ab_w_out, at_w_qkv, at_q_gain, at_k_gain, at_sink, at_w_o):
    for i in range(DEPTH):
        last = i == DEPTH - 1
        j = i // 2
        mod_x = (jax.nn.silu(c) @ w_mod[i] + b_mod[i])[:, None, :]
        mod_c = (jax.nn.silu(c_ctx) @ w_mod[i] + b_mod[i])[None, None, :]
        sh1x, sc1x, g1x, sh2x, sc2x, g2x = jnp.split(mod_x, 6, axis=-1)
        sh1c, sc1c, g1c, sh2c, sc2c, g2c = jnp.split(mod_c, 6, axis=-1)
        hx = rms_norm(x, norm1[i]) * (1.0 + sc1x) + sh1x
        hc = rms_norm(ctx, norm1[i]) * (1.0 + sc1c) + sh1c
        if i % 2 == 0:
            filt = (hy_f_w1[j], hy_f_b1[j], hy_f_w2[j], hy_f_b2[j], hy_f_w3[j], hy_f_freq[j])
            lru_params = (lru_w_a[j], lru_b_a[j], lru_w_i[j], lru_b_i[j], lru_lam[j])
            mx, mc = hyena_lru_mixer(hx, hc, ab_w_in[j], hy_conv_w[j], hy_conv_b[j], filt, hy_bias[j],
                                     lru_conv_w[j], lru_conv_b[j], lru_params, ab_w_out[j], not last)
        else:
            mx, mc = window_attention(hx, hc, at_w_qkv[j], at_q_gain[j], at_k_gain[j], at_sink[j],
                                      at_w_o[j], not last)
        x = x + g1x * mx
        x = x + g2x * swiglu(rms_norm(x, norm2[i]) * (1.0 + sc2x) + sh2x, ffn_w1[i], ffn_w3[i], ffn_w2[i])
        if not last:
            ctx = ctx + g1c * mc
            ctx = ctx + g2c * swiglu(rms_norm(ctx, norm2[i]) * (1.0 + sc2c) + sh2c,
                                     ffn_w1[i], ffn_w3[i], ffn_w2[i])
    return x
```

```python
import math
import numpy as np
import ml_dtypes
import concourse.bass as bass
import concourse.mybir as mybir
from concourse.bass_utils import run_bass_kernel_spmd
from contextlib import ExitStack

F32 = mybir.dt.float32
BF16 = mybir.dt.bfloat16
AF = mybir.ActivationFunctionType
ALU = mybir.AluOpType

D = 1024
S = 8192
C = 256
KC = 8
DFF = 2816
FC = 22
EPS = 1e-6
NB = 512
NK1 = 68
NG = 17


class Sch:
    CH = 30000
    POOL = {'sp': 40, 'act': 8, 'pool': 24}

    def __init__(self, nc, es):
        self.nc, self.es = nc, es
        self.E = {'pe': nc.tensor, 'dve': nc.vector, 'act': nc.scalar, 'pool': nc.gpsimd, 'sp': nc.sync}
        self.n = {e: 0 for e in self.E}
        self.csem = {e: [] for e in self.E}
        self.seen = {e: {} for e in self.E}
        self.lastw = {}
        self.rd = {}
        self.dpool = {q: [] for q in self.POOL}
        self.dnext = {q: 0 for q in self.POOL}
        self.semobj = []
        self.ninst = 0

    def _newsem(self, name):
        s = self.es.enter_context(self.nc.semaphore(name))
        self.semobj.append(s)
        return len(self.semobj) - 1

    def _wait(self, e, tok):
        sid, val = tok
        if self.seen[e].get(sid, 0) >= val:
            return
        self.E[e].wait_ge(self.semobj[sid], val)
        self.seen[e][sid] = val

    def _deps(self, e, reads, writes):
        toks = {}

        def add(t):
            if t is not None and toks.get(t[0], 0) < t[1]:
                toks[t[0]] = t[1]
        for r in reads:
            add(self.lastw.get(r))
        for w in writes:
            add(self.lastw.get(w))
            for sid, v in self.rd.get(w, {}).items():
                add((sid, v))
        own = set(self.csem[e]) if e == 'pe' else ()
        for sid, v in toks.items():
            if sid in own:
                continue
            self._wait(e, (sid, v))

    def _record(self, tok, reads, writes):
        for r in reads:
            d = self.rd.setdefault(r, {})
            if d.get(tok[0], 0) < tok[1]:
                d[tok[0]] = tok[1]
        for w in writes:
            self.lastw[w] = tok
            self.rd[w] = {}

    def op(self, e, fn, reads=(), writes=()):
        self._deps(e, reads, writes)
        ins = fn()
        k = self.n[e]
        ci = k // self.CH
        if ci >= len(self.csem[e]):
            self.csem[e].append(self._newsem("c_%s_%d" % (e, ci)))
        sid = self.csem[e][ci]
        ins.then_inc(self.semobj[sid], 1)
        self.n[e] += 1
        self.ninst += 1
        tok = (sid, k % self.CH + 1)
        self._record(tok, reads, writes)
        return tok

    def dma(self, q, out, in_, reads=(), writes=(), **kw):
        self._deps(q, reads, writes)
        pool = self.dpool[q]
        if len(pool) < self.POOL[q]:
            pool.append([self._newsem("d_%s_%d" % (q, len(pool))), 0])
            idx = len(pool) - 1
        else:
            idx = self.dnext[q] % self.POOL[q]
        self.dnext[q] += 1
        sid, v = pool[idx]
        if v > 0:
            self._wait(q, (sid, v))
        ins = self.E[q].dma_start(out=out, in_=in_, **kw)
        ins.then_inc(self.semobj[sid], 16)
        pool[idx][1] = v + 16
        self.ninst += 1
        tok = (sid, v + 16)
        self._record(tok, reads, writes)
        return tok

    def barrier(self):
        toks = []
        for e in self.E:
            if self.n[e] > 0:
                k = self.n[e] - 1
                toks.append((self.csem[e][k // self.CH], k % self.CH + 1))
        for q, pool in self.dpool.items():
            for sid, v in pool:
                if v > 0:
                    toks.append((sid, v))
        for e in self.E:
            for t in toks:
                self._wait(e, t)
        self.lastw.clear()
        self.rd.clear()


SMALL_ITEMS = [('c', 16), ('norm1', 16), ('norm2', 16), ('bmod', 96), ('hy_cw', 36), ('hy_cb', 12),
               ('hy_bias', 8), ('lru_cw', 16), ('lru_cb', 4), ('lru_ba', 8), ('lru_bi', 8), ('lru_lam', 8),
               ('qgain', 1), ('kgain', 1), ('sink', 16), ('hyf_b1', 1), ('hyf_b2', 1), ('hyf_freq', 1)]


def small_offsets():
    off = {}
    o = 0
    for k, n in SMALL_ITEMS:
        off[k] = (o, n)
        o += n
    return off, o


def pk(v, nch):
    return np.ascontiguousarray(np.asarray(v, np.float32).reshape(nch, 128).T)


def build_small(inp, b):
    off, tot = small_offsets()
    sm = np.zeros((128, tot), np.float32)

    def put(name, arr):
        o, n = off[name]
        assert arr.shape == (128, n), (name, arr.shape, n)
        sm[:, o:o + n] = arr
    cc = np.zeros((128, 16), np.float32)
    cc[:, 0::2] = pk(inp['c'][b], 8)
    cc[:, 1::2] = pk(inp['c_ctx'], 8)
    put('c', cc)
    put('norm1', np.concatenate([pk(inp['norm1'][i], 8) for i in range(2)], 1))
    put('norm2', np.concatenate([pk(inp['norm2'][i], 8) for i in range(2)], 1))
    put('bmod', np.concatenate([pk(inp['b_mod'][i], 48) for i in range(2)], 1))
    cw = inp['hy_conv_w'][0]
    a = np.zeros((128, 12, 3), np.float32)
    for j in range(3):
        a[:, :, j] = pk(cw[j], 12)
    put('hy_cw', a.reshape(128, 36))
    put('hy_cb', pk(inp['hy_conv_b'][0], 12))
    put('hy_bias', np.concatenate([pk(inp['hy_bias'][0, n], 4) for n in range(2)], 1))
    lw = inp['lru_conv_w'][0]
    a = np.zeros((128, 4, 4), np.float32)
    for j in range(4):
        a[:, :, j] = pk(lw[j], 4)
    put('lru_cw', a.reshape(128, 16))
    put('lru_cb', pk(inp['lru_conv_b'][0], 4))
    put('lru_ba', np.concatenate([pk(inp['lru_b_a'][0, d], 4) for d in range(2)], 1))
    put('lru_bi', np.concatenate([pk(inp['lru_b_i'][0, d], 4) for d in range(2)], 1))
    put('lru_lam', np.concatenate([pk(inp['lru_lam'][0, d], 4) for d in range(2)], 1))
    put('qgain', np.tile(np.asarray(inp['at_q_gain'][0], np.float32), 2)[:, None])
    put('kgain', np.tile(np.asarray(inp['at_k_gain'][0], np.float32), 2)[:, None])
    put('sink', np.tile(np.asarray(inp['at_sink'][0], np.float32)[None, :], (128, 1)))
    for nm, key in (('hyf_b1', 'hy_f_b1'), ('hyf_b2', 'hy_f_b2'), ('hyf_freq', 'hy_f_freq')):
        v = np.zeros((128, 1), np.float32)
        v[:64, 0] = inp[key][0]
        put(nm, v)
    return sm


def hy_params(L):
    N = 2 * L
    N2 = N // 128
    NH = N2 // 2
    K1 = N2 // 2 + 1
    NK1 = ((K1 + 3) // 4) * 4
    FB = min(512, L)
    return dict(L=L, N=N, N2=N2, NH=NH, K1=K1, NK1=NK1, NG=NK1 // 4, FB=FB, NBLK=N // FB)


def bf(a):
    return np.ascontiguousarray(np.asarray(a, np.float32).astype(ml_dtypes.bfloat16))


def hy_consts(L, pre):
    P = hy_params(L)
    N, N2, NH, NK1, FB, NBLK = P['N'], P['N2'], P['NH'], P['NK1'], P['FB'], P['NBLK']
    c = {}
    n2 = np.arange(N2)[:, None]
    k1 = np.arange(NK1)[None, :]
    th = 2 * np.pi * n2 * k1 / N2
    f1 = np.stack([np.sin(th), np.cos(th), -np.sin(th)], -1).reshape(N2, NK1 * 3)
    c[pre + 'f1tab'] = bf(f1)
    n1 = np.arange(128)[:, None, None]
    kk = np.arange(NK1)[None, :, None] + N2 * np.arange(128)[None, None, :]
    th = 2 * np.pi * ((n1 * kk) % N) / N
    c[pre + 'gtab'] = bf(np.stack([np.cos(th), -np.sin(th)], 1))
    tht = np.transpose(th, (2, 1, 0))
    c[pre + 'gttab'] = bf(np.stack([np.cos(tht), np.sin(tht)], 1))
    w = np.zeros(NK1)
    w[0] = 1.0
    w[N2 // 2] = 1.0
    w[1:N2 // 2] = 2.0
    ph = 2 * np.pi * np.arange(NK1)[:, None] * np.arange(NH)[None, :] / N2
    e = np.stack([np.cos(ph), -np.sin(ph)], 1) * (w[:, None, None] / N)
    c[pre + 'etab'] = bf(e)
    p = np.arange(N)
    lag = np.where(p < L, p, N - p).astype(np.float64)
    lag[L] = 0
    t = lag / (L - 1)
    bands = np.linspace(1e-4, 15.0, 16)
    wv = 2 * np.pi * lag / L
    z = np.concatenate([t[None, :], np.cos(bands[:, None] * wv[None, :]), -np.sin(bands[:, None] * wv[None, :])], 0)
    c[pre + 'zfeat'] = np.ascontiguousarray(z.astype(np.float32))
    min_decay = math.log(1e-2) / 1.5
    max_decay = math.log(1e-2) / 0.3
    deltas = np.abs(np.linspace(min_decay, max_decay, 512)) / (L - 1)
    c[pre + 'ndelta'] = pk(-deltas, 4)
    lagmin = np.array([(j * FB) if (j * FB) < L else (N - j * FB - FB + 1) for j in range(NBLK)], np.float32)
    c[pre + 'lagmin'] = np.ascontiguousarray(np.tile(lagmin[None, :], (128, 1)))
    c[pre + 'iota'] = np.ascontiguousarray(np.tile(np.arange(FB, dtype=np.float32)[None, :], (128, 1)))
    return c


def att_consts():
    c = {}
    p = np.arange(128)
    d = p % 64
    i = d % 16
    inv = 10000.0 ** (-(i.astype(np.float64)) / 16.0)
    t = np.arange(S)
    row = t // 64
    col = t % 64
    pos = np.where((d < 32)[:, None], row[None, :], col[None, :]).astype(np.float64)
    ang = pos * inv[:, None]
    c['rope_cos'] = np.ascontiguousarray(np.cos(ang).astype(np.float32))
    c['rope_sin'] = np.ascontiguousarray(np.sin(ang).astype(np.float32))
    R = np.zeros((128, 128), np.float32)
    for dst in range(128):
        if (dst % 32) < 16:
            R[dst + 16, dst] = -1.0
        else:
            R[dst - 16, dst] = 1.0
    c['rope_R'] = bf(R)
    bd = np.zeros((128, 128), np.float32)
    bd[:64, :64] = 1.0 / 64
    bd[64:, 64:] = 1.0 / 64
    c['bd64'] = bf(bd)
    k = np.arange(128)[:, None]
    q = np.arange(128)[None, :]
    m = np.stack([(k >= q), (k <= q)], 1).astype(np.float32)
    c['att_mask'] = bf(m)
    c['ident'] = bf(np.eye(128, dtype=np.float32))
    return c


_CONSTS = None


def make_consts():
    global _CONSTS
    if _CONSTS is None:
        c = {}
        c.update(hy_consts(S, 'hx_'))
        c.update(hy_consts(C, 'hc_'))
        c.update(att_consts())
        _CONSTS = c
    return _CONSTS


class KB:
    def __init__(self, dbg=()):
        self.dbg = set(dbg)
        self.nc = bass.Bass("TRN2", target_bir_lowering=False)
        self.es = ExitStack()
        self.sch = None
        self.din = {}
        self.dout = {}

    def inp(self, name, shape, dt=F32):
        t = self.nc.dram_tensor(name, list(shape), dt, kind="ExternalInput").ap()
        self.din[name] = t
        return t

    def scratch(self, name, shape, dt):
        kind = "ExternalOutput" if name in self.dbg else "Internal"
        t = self.nc.dram_tensor(name, list(shape), dt, kind=kind).ap()
        if name in self.dbg:
            self.dout[name] = t
        return t

    def sb(self, st, name, shape, dt):
        self.uid = getattr(self, 'uid', 0) + 1
        return st.enter_context(self.nc.sbuf_tensor("%s_u%d" % (name, self.uid), list(shape), dt))


def build_program(dbg=(), stages=None):
    kb = KB(dbg)
    nc = kb.nc
    off, nsm = small_offsets()
    xT = kb.inp("xT", [D, S])
    ctxT = kb.inp("ctxT", [D, C])
    smallp = kb.inp("smallp", [128, nsm])
    w_mod = kb.inp("w_mod", [2, D, 6 * D])
    ffn_w1 = kb.inp("ffn_w1", [2, D, DFF])
    ffn_w3 = kb.inp("ffn_w3", [2, D, DFF])
    ffn_w2 = kb.inp("ffn_w2", [2, DFF, D])
    ab_w_in = kb.inp("ab_w_in", [D, 2560])
    ab_w_out = kb.inp("ab_w_out", [D, D])
    lru_w_a = kb.inp("lru_w_a", [2, 8, 64, 64])
    lru_w_i = kb.inp("lru_w_i", [2, 8, 64, 64])
    hy_f_w1 = kb.inp("hy_f_w1", [33, 64])
    hy_f_w2 = kb.inp("hy_f_w2", [64, 64])
    hy_f_w3 = kb.inp("hy_f_w3", [64, 2048])
    at_w_qkv = kb.inp("at_w_qkv", [D, 1536])
    at_w_o = kb.inp("at_w_o", [D, D])
    ident_d = kb.inp("ident", [128, 128], BF16)
    outT = nc.dram_tensor("outT", [D, S], F32, kind="ExternalOutput").ap()
    kb.dout["outT"] = outT
    px = kb.scratch("px", [2560, S], BF16)
    pc = kb.scratch("pc", [2560, C], BF16)
    ymix = kb.scratch("ymix", [D, S], BF16)
    ymixc = kb.scratch("ymixc", [D, C], BF16)
    xb0 = kb.scratch("xb0", [D, S], F32)
    ctxb0 = kb.scratch("ctxb0", [D, C], F32)
    qkraw = kb.scratch("qkraw", [1280, S], BF16)
    qkraw_c = kb.scratch("qkraw_c", [1280, C], BF16)
    qr = kb.scratch("qr", [1024, S], BF16)
    kr = kb.scratch("kr", [256, S], BF16)
    kcr = kb.scratch("kcr", [256, C], BF16)
    vtok = kb.scratch("vtok", [S, 260], BF16)
    vctok = kb.scratch("vctok", [C, 260], BF16)
    oT = kb.scratch("oT", [D, S], BF16)

    with kb.es as es:
        sch = Sch(nc, es)
        kb.sch = sch
        sm = kb.sb(es, "sm", [128, nsm], F32)
        dsc = kb.sb(es, "dsc", [128, 2, 2, 6, 8], F32)
        ones_bf = kb.sb(es, "ones_bf", [128, 128], BF16)
        epsc = kb.sb(es, "epsc", [128, 1], F32)
        ps = [es.enter_context(nc.psum_tensor("ps%d" % i, [128, 512], F32)) for i in range(8)]
        st = {'psi': 0}

        def nextps():
            i = st['psi']
            st['psi'] = (i + 1) % 8
            return i

        sch.dma('sp', sm[:], smallp, writes=['sm'])
        sch.op('dve', lambda: nc.vector.memset(ones_bf[:], 1.0 / D), writes=['ones_bf'])
        sch.op('dve', lambda: nc.vector.memset(epsc[:], EPS), writes=['epsc'])

        def smc(name, j0=0, n=None):
            o, nn = off[name]
            if n is None:
                n = nn - j0
            return sm[:, o + j0:o + j0 + n]

        def phase_mod():
            with ExitStack() as ph:
                sc = kb.sb(ph, "p0_sc", [128, 16], F32)
                modv = kb.sb(ph, "p0_modv", [128, 2, 48, 2], F32)
                wp = [kb.sb(ph, "p0_wp%d" % i, [128, 8, 768], F32) for i in range(2)]
                sch.op('act', lambda: nc.scalar.activation(out=sc[:], in_=smc('c'), func=AF.Silu),
                       reads=['sm'], writes=['p0_sc'])
                cnt = 0
                for i in range(2):
                    pbank = nextps()
                    for pn in range(8):
                        w = wp[cnt % 2]
                        wk = 'p0_wp%d' % (cnt % 2)
                        cnt += 1
                        sch.dma('sp', w[:], w_mod[i][:, pn * 768:(pn + 1) * 768].rearrange("(k p) n -> p k n", p=128),
                                writes=[wk])
                        for ml in range(6):
                            m = pn * 6 + ml
                            for k in range(8):
                                sch.op('pe', lambda w=w, ml=ml, k=k, m=m, pbank=pbank: nc.tensor.matmul(
                                    out=ps[pbank][:, 2 * m:2 * m + 2], lhsT=w[:, k, ml * 128:(ml + 1) * 128],
                                    rhs=sc[:, 2 * k:2 * k + 2], start=(k == 0), stop=(k == 7)),
                                    reads=[wk, 'p0_sc'], writes=[('ps', pbank)])
                    o, _ = off['bmod']
                    sch.op('dve', lambda i=i, pbank=pbank, o=o: nc.vector.tensor_tensor(
                        out=modv[:, i, :, :], in0=ps[pbank][:, 0:96].rearrange("p (m s) -> p m s", s=2),
                        in1=sm[:, o + i * 48:o + (i + 1) * 48].unsqueeze(2).to_broadcast([128, 48, 2]), op=ALU.add),
                        reads=[('ps', pbank), 'sm'], writes=[('modv', i)])
                    for s in range(2):
                        n1 = smc('norm1', i * 8, 8)
                        n2 = smc('norm2', i * 8, 8)
                        rd = [('modv', i), 'sm']
                        sch.op('dve', lambda i=i, s=s, n1=n1: nc.vector.scalar_tensor_tensor(
                            out=dsc[:, i, s, 0, :], in0=modv[:, i, 8:16, s], scalar=1.0, in1=n1, op0=ALU.add, op1=ALU.mult),
                            reads=rd, writes=['dsc'])
                        sch.op('dve', lambda i=i, s=s, n2=n2: nc.vector.scalar_tensor_tensor(
                            out=dsc[:, i, s, 3, :], in0=modv[:, i, 32:40, s], scalar=1.0, in1=n2, op0=ALU.add, op1=ALU.mult),
                            reads=rd, writes=['dsc'])
                        for kind, c0 in ((1, 0), (2, 16), (4, 24), (5, 40)):
                            sch.op('dve', lambda i=i, s=s, kind=kind, c0=c0: nc.vector.tensor_copy(
                                out=dsc[:, i, s, kind, :], in_=modv[:, i, c0:c0 + 8, s]),
                                reads=rd, writes=['dsc'])
                sch.barrier()

        def load_w_bf16(dst, dst_key, src, nk, ncols):
            for k in range(nk):
                sch.dma('pool', dst[:, k, :], src[k * 128:(k + 1) * 128, :], writes=[(dst_key, k)],
                        max_dma_last_dim=2048)

        def norm_mod(xin, xkey, n, h, hkey, sq, sqkey, rstd, rkey, layer, stream, part, tmps, tkey):
            kA = 0 if part == 1 else 3
            for k in range(KC):
                sch.op('act', lambda k=k: nc.scalar.activation(out=sq[:, k, :n], in_=xin[:, k, :n], func=AF.Square),
                       reads=[(xkey, k)], writes=[(sqkey, k)])
            pb = nextps()
            for k in range(KC):
                sch.op('pe', lambda k=k, pb=pb: nc.tensor.matmul(out=ps[pb][:, :n], lhsT=ones_bf[:], rhs=sq[:, k, :n],
                                                                 start=(k == 0), stop=(k == KC - 1)),
                       reads=[(sqkey, k), 'ones_bf'], writes=[('ps', pb)])
            sch.op('act', lambda pb=pb: nc.scalar.activation(out=rstd[:, :n], in_=ps[pb][:, :n], func=AF.Ln,
                                                             bias=epsc[:, 0:1], scale=1.0),
                   reads=[('ps', pb), 'epsc'], writes=[rkey])
            sch.op('act', lambda: nc.scalar.activation(out=rstd[:, :n], in_=rstd[:, :n], func=AF.Exp, scale=-0.5),
                   reads=[rkey], writes=[rkey])
            for k in range(KC):
                tt = tmps[k % 2]
                tk = tkey + str(k % 2)
                sch.op('dve', lambda k=k, tt=tt: nc.vector.tensor_tensor(
                    out=tt[:, :n], in0=xin[:, k, :n], in1=rstd[:, :n], op=ALU.mult),
                    reads=[(xkey, k), rkey], writes=[tk])
                sch.op('act', lambda k=k, tt=tt: nc.scalar.activation(
                    out=h[:, k, :n], in_=tt[:, :n], func=AF.Identity,
                    bias=dsc[:, layer, stream, kA + 1, k:k + 1], scale=dsc[:, layer, stream, kA, k:k + 1]),
                    reads=[tk, 'dsc'], writes=[(hkey, k)])

        def phase_x1(layer, w_src, nout, xsrc, csrc, dst_x, dst_c, col0, gs=4, wload=None, vproj=None):
            nm = nout // 128
            with ExitStack() as ph:
                w = kb.sb(ph, "x1_w", [128, KC, nout], BF16)
                xin = kb.sb(ph, "x1_xin", [128, KC, NB], F32)
                sq = kb.sb(ph, "x1_sq", [128, KC, NB], BF16)
                h = kb.sb(ph, "x1_h", [128, KC, NB], BF16)
                rstd = kb.sb(ph, "x1_rstd", [128, NB], F32)
                tmps = [kb.sb(ph, "x1_tmp%d" % i, [128, NB], F32) for i in range(2)]
                ob = [kb.sb(ph, "x1_ob%d" % i, [128, gs, NB], BF16) for i in range(2)]
                if wload is None:
                    load_w_bf16(w, 'x1_w', w_src, KC, nout)
                else:
                    wload(w)
                if vproj is not None:
                    wv = kb.sb(ph, "x1_wv", [128, KC, 256], BF16)
                    vb = [kb.sb(ph, "x1_vb%d" % i, [128, 4, 65], BF16) for i in range(2)]
                    for k in range(KC):
                        sch.dma('pool', wv[:, k, :], vproj[0][k * 128:(k + 1) * 128, 1280:1536], writes=[('x1_wv', k)])
                    for i in range(2):
                        sch.op('dve', lambda i=i: nc.vector.memset(vb[i][:], 1.0), writes=['x1_vb%d' % i])
                    vcnt = 0
                blocks = [(1, csrc, dst_c, 0, C)] + [(0, xsrc, dst_x, j * NB, NB) for j in range(S // NB)]
                oc = 0
                for (stream, src, dst, t0, n) in blocks:
                    for k in range(KC):
                        sch.dma('sp', xin[:, k, :n], src[k * 128:(k + 1) * 128, t0:t0 + n], writes=[('x1_xin', k)])
                    norm_mod(xin, 'x1_xin', n, h, 'x1_h', sq, 'x1_sq', rstd, 'x1_rstd', layer, stream, 1, tmps, 'x1_tmp')
                    if vproj is not None:
                        vdst = vproj[2] if stream == 1 else vproj[1]
                        for sub in range(n // 128):
                            pb = nextps()
                            for k in range(KC):
                                sch.op('pe', lambda k=k, pb=pb, sub=sub: nc.tensor.matmul(
                                    out=ps[pb][:, 0:256], lhsT=h[:, k, sub * 128:(sub + 1) * 128], rhs=wv[:, k, :],
                                    start=(k == 0), stop=(k == KC - 1)),
                                    reads=[('x1_wv', k), ('x1_h', k)], writes=[('ps', pb)])
                            v_ = vb[vcnt % 2]
                            vk = 'x1_vb%d' % (vcnt % 2)
                            vcnt += 1
                            sch.op('dve', lambda v_=v_, pb=pb: nc.vector.tensor_copy(
                                out=v_[:, :, 0:64], in_=ps[pb][:, 0:256].rearrange("p (g d) -> p g d", d=64)),
                                reads=[('ps', pb)], writes=[vk])
                            sch.dma('pool', vdst[t0 + sub * 128:t0 + (sub + 1) * 128, :], v_[:].rearrange("p g d -> p (g d)"), reads=[vk])
                    for mg in range(nm // gs):
                        o = ob[oc % 2]
                        okey = 'x1_ob%d' % (oc % 2)
                        oc += 1
                        for ml in range(gs):
                            m = mg * gs + ml
                            pb = nextps()
                            for k in range(KC):
                                sch.op('pe', lambda k=k, m=m, pb=pb: nc.tensor.matmul(
                                    out=ps[pb][:, :n], lhsT=w[:, k, m * 128:(m + 1) * 128], rhs=h[:, k, :n],
                                    start=(k == 0), stop=(k == KC - 1)),
                                    reads=[('x1_w', k), ('x1_h', k)], writes=[('ps', pb)])
                            if ml % 2 == 0:
                                sch.op('dve', lambda o=o, ml=ml, pb=pb: nc.vector.tensor_copy(out=o[:, ml, :n], in_=ps[pb][:, :n]),
                                       reads=[('ps', pb)], writes=[(okey, ml)])
                            else:
                                sch.op('act', lambda o=o, ml=ml, pb=pb: nc.scalar.copy(out=o[:, ml, :n], in_=ps[pb][:, :n]),
                                       reads=[('ps', pb)], writes=[(okey, ml)])
                        sch.dma('pool', dst[mg * gs * 128:(mg + 1) * gs * 128, col0 + t0:col0 + t0 + n].rearrange("(m p) t -> p m t", p=128),
                                o[:, :, :n], reads=[(okey, ml) for ml in range(gs)])
                sch.barrier()

        def phase_x2(layer, wo_src, y_x, y_c, xsrc, csrc, dst_x, dst_c, do_ctx):
            with ExitStack() as ph:
                wo = kb.sb(ph, "x2_wo", [128, KC, D], BF16)
                w1 = kb.sb(ph, "x2_w1", [128, KC, DFF], BF16)
                w3 = kb.sb(ph, "x2_w3", [128, KC, DFF], BF16)
                w2 = kb.sb(ph, "x2_w2", [128, FC, D], BF16)
                xin = kb.sb(ph, "x2_xin", [128, KC, NB], F32)
                yb = kb.sb(ph, "x2_y", [128, KC, NB], BF16)
                u = kb.sb(ph, "x2_u", [128, FC, NB], BF16)
                sl = [kb.sb(ph, "x2_sl%d" % i, [128, NB], F32) for i in range(2)]
                rstd = kb.sb(ph, "x2_rstd", [128, NB], F32)
                sq = u
                load_w_bf16(wo, 'x2_wo', wo_src, KC, D)
                load_w_bf16(w1, 'x2_w1', ffn_w1[layer], KC, DFF)
                load_w_bf16(w3, 'x2_w3', ffn_w3[layer], KC, DFF)
                load_w_bf16(w2, 'x2_w2', ffn_w2[layer], FC, D)
                blocks = [(0, xsrc, y_x, dst_x, j * NB, NB) for j in range(S // NB)]
                if do_ctx:
                    blocks = [(1, csrc, y_c, dst_c, 0, C)] + blocks
                slc = 0
                for (stream, src, ysrc, dst, t0, n) in blocks:
                    for k in range(KC):
                        sch.dma('sp', xin[:, k, :n], src[k * 128:(k + 1) * 128, t0:t0 + n], writes=[('x2_xin', k)])
                    sch.dma('sp', yb[:, :, :n], ysrc[:, t0:t0 + n].rearrange("(k p) t -> p k t", p=128),
                            writes=[('x2_y', k) for k in range(KC)])
                    for m in range(KC):
                        pb = nextps()
                        for k in range(KC):
                            sch.op('pe', lambda k=k, m=m, pb=pb: nc.tensor.matmul(
                                out=ps[pb][:, :n], lhsT=wo[:, k, m * 128:(m + 1) * 128], rhs=yb[:, k, :n],
                                start=(k == 0), stop=(k == KC - 1)),
                                reads=[('x2_wo', k), ('x2_y', k)], writes=[('ps', pb)])
                        sch.op('dve', lambda m=m, pb=pb: nc.vector.scalar_tensor_tensor(
                            out=xin[:, m, :n], in0=ps[pb][:, :n], scalar=dsc[:, layer, stream, 2, m:m + 1],
                            in1=xin[:, m, :n], op0=ALU.mult, op1=ALU.add),
                            reads=[('ps', pb), ('x2_xin', m), 'dsc'], writes=[('x2_xin', m)])
                    h = yb
                    norm_mod(xin, 'x2_xin', n, h, 'x2_y', sq, 'x2_u', rstd, 'x2_rstd', layer, stream, 2, sl, 'x2_sl')
                    for f in range(FC):
                        pb1 = nextps()
                        for k in range(KC):
                            sch.op('pe', lambda k=k, f=f, pb1=pb1: nc.tensor.matmul(
                                out=ps[pb1][:, :n], lhsT=w1[:, k, f * 128:(f + 1) * 128], rhs=h[:, k, :n],
                                start=(k == 0), stop=(k == KC - 1)),
                                reads=[('x2_w1', k), ('x2_y', k)], writes=[('ps', pb1)])
                        pb3 = nextps()
                        for k in range(KC):
                            sch.op('pe', lambda k=k, f=f, pb3=pb3: nc.tensor.matmul(
                                out=ps[pb3][:, :n], lhsT=w3[:, k, f * 128:(f + 1) * 128], rhs=h[:, k, :n],
                                start=(k == 0), stop=(k == KC - 1)),
                                reads=[('x2_w3', k), ('x2_y', k)], writes=[('ps', pb3)])
                        s_ = sl[slc % 2]
                        skey = 'x2_sl%d' % (slc % 2)
                        slc += 1
                        sch.op('act', lambda s_=s_, pb1=pb1: nc.scalar.activation(out=s_[:, :n], in_=ps[pb1][:, :n], func=AF.Silu),
                               reads=[('ps', pb1)], writes=[skey])
                        sch.op('dve', lambda s_=s_, pb3=pb3, f=f: nc.vector.tensor_tensor(
                            out=u[:, f, :n], in0=ps[pb3][:, :n], in1=s_[:, :n], op=ALU.mult),
                            reads=[('ps', pb3), skey], writes=[('x2_u', f)])
                    for m in range(KC):
                        pb = nextps()
                        for f in range(FC):
                            sch.op('pe', lambda f=f, m=m, pb=pb: nc.tensor.matmul(
                                out=ps[pb][:, :n], lhsT=w2[:, f, m * 128:(m + 1) * 128], rhs=u[:, f, :n],
                                start=(f == 0), stop=(f == FC - 1)),
                                reads=[('x2_w2', f), ('x2_u', f)], writes=[('ps', pb)])
                        sch.op('dve', lambda m=m, pb=pb: nc.vector.scalar_tensor_tensor(
                            out=xin[:, m, :n], in0=ps[pb][:, :n], scalar=dsc[:, layer, stream, 5, m:m + 1],
                            in1=xin[:, m, :n], op0=ALU.mult, op1=ALU.add),
                            reads=[('ps', pb), ('x2_xin', m), 'dsc'], writes=[('x2_xin', m)])
                        sch.dma('pool', dst[m * 128:(m + 1) * 128, t0:t0 + n], xin[:, m, :n], reads=[('x2_xin', m)])
                sch.barrier()


        def gelu_tanh(src, n, t1, t1k, t2, t2k, srck):
            sch.op('act', lambda: nc.scalar.activation(out=t1[:, :n], in_=src, func=AF.Square), reads=srck, writes=t1k)
            sch.op('dve', lambda: nc.vector.tensor_scalar(out=t1[:, :n], in0=t1[:, :n], scalar1=0.044715, scalar2=1.0,
                                                          op0=ALU.mult, op1=ALU.add), reads=t1k, writes=t1k)
            sch.op('dve', lambda: nc.vector.tensor_tensor(out=t1[:, :n], in0=t1[:, :n], in1=src, op=ALU.mult),
                   reads=t1k + srck, writes=t1k)
            sch.op('act', lambda: nc.scalar.activation(out=t1[:, :n], in_=t1[:, :n], func=AF.Sigmoid, scale=1.5957691216057308),
                   reads=t1k, writes=t1k)
            sch.op('dve', lambda: nc.vector.tensor_tensor(out=t2[:, :n], in0=t1[:, :n], in1=src, op=ALU.mult),
                   reads=t1k + srck, writes=t2k)

        def phase_lru():
            T = 2048
            NSEG = S // T
            with ExitStack() as ph:
                cA = kb.sb(ph, "lr_cA", [128, 2, 8], F32)
                wblk = kb.sb(ph, "lr_w", [128, 2, 2, 128], F32)
                identb = kb.sb(ph, "lr_id", [128, 128], BF16)
                dg = kb.sb(ph, "lr_dg", [128, 4, 128], BF16)
                rx = kb.sb(ph, "lr_rx", [128, S], F32)
                hcs = kb.sb(ph, "lr_hcs", [128, 2, C], F32)
                carry = kb.sb(ph, "lr_carry", [128, 2], F32)
                BS = []
                for i in range(3):
                    BS.append(dict(
                        i=i,
                        raw=kb.sb(ph, "lr_raw%d" % i, [128, T + 3], BF16),
                        lxc=kb.sb(ph, "lr_lxc%d" % i, [128, T], F32),
                        A=kb.sb(ph, "lr_A%d" % i, [128, T], F32),
                        B=kb.sb(ph, "lr_B%d" % i, [128, T], F32),
                        tmp=kb.sb(ph, "lr_tmp%d" % i, [128, T], F32),
                        gx=kb.sb(ph, "lr_gx%d" % i, [128, T], BF16),
                        ob=kb.sb(ph, "lr_ob%d" % i, [128, T], BF16)))
                sch.dma('sp', identb[:], ident_d, writes=['lr_id'])
                lam = smc('lru_lam')
                sch.op('act', lambda: nc.scalar.activation(out=cA[:, 0, :], in_=lam, func=AF.Exp, scale=-1.0),
                       reads=['sm'], writes=['lr_cA'])
                sch.op('act', lambda: nc.scalar.activation(out=cA[:, 0, :], in_=cA[:, 0, :], func=AF.Ln, bias=1.0, scale=1.0),
                       reads=['lr_cA'], writes=['lr_cA'])
                sch.op('dve', lambda: nc.vector.tensor_scalar(out=cA[:, 1, :], in0=cA[:, 0, :], scalar1=-16.0, scalar2=None,
                                                              op0=ALU.mult), reads=['lr_cA'], writes=['lr_cA'])
                sch.op('dve', lambda: nc.vector.tensor_scalar(out=cA[:, 0, :], in0=cA[:, 0, :], scalar1=-8.0, scalar2=None,
                                                              op0=ALU.mult), reads=['lr_cA'], writes=['lr_cA'])
                ocw, _ = off['lru_cw']
                ocb, _ = off['lru_cb']
                oa, _ = off['lru_ba']
                oi, _ = off['lru_bi']
                cnt = [0]

                def K(bs, nm):
                    return 'lr_%s%d' % (nm, bs['i'])

                def conv4(bs, q, n):
                    raw, lxc = bs['raw'], bs['lxc']
                    for b0 in range(0, n, NB):
                        nn = min(NB, n - b0)
                        pb = nextps()
                        for j in range(4):
                            sch.op('pe', lambda pb=pb, j=j, b0=b0, nn=nn: nc.tensor.matmul(
                                out=ps[pb][:, :nn], lhsT=dg[:, j, :], rhs=raw[:, j + b0:j + b0 + nn], start=(j == 0), stop=(j == 3)),
                                reads=[K(bs, 'raw'), 'lr_dg'], writes=[('ps', pb)])
                        sch.op('act', lambda pb=pb, b0=b0, nn=nn: nc.scalar.activation(
                            out=lxc[:, b0:b0 + nn], in_=ps[pb][:, :nn], func=AF.Identity, bias=sm[:, ocb + q:ocb + q + 1], scale=1.0),
                            reads=[('ps', pb), 'sm'], writes=[(K(bs, 'lxc'), b0)])
                    return [(K(bs, 'lxc'), b0) for b0 in range(0, n, NB)]

                def gates(bs, q, d, n):
                    lxc, A, Bt = bs['lxc'], bs['A'], bs['B']
                    for b0 in range(0, n, NB):
                        nn = min(NB, n - b0)
                        pa = nextps()
                        sch.op('pe', lambda pa=pa, b0=b0, nn=nn: nc.tensor.matmul(
                            out=ps[pa][:, :nn], lhsT=wblk[:, d, 0, :], rhs=lxc[:, b0:b0 + nn], start=True, stop=True),
                            reads=['lr_w', (K(bs, 'lxc'), b0)], writes=[('ps', pa)])
                        pi = nextps()
                        sch.op('pe', lambda pi=pi, b0=b0, nn=nn: nc.tensor.matmul(
                            out=ps[pi][:, :nn], lhsT=wblk[:, d, 1, :], rhs=lxc[:, b0:b0 + nn], start=True, stop=True),
                            reads=['lr_w', (K(bs, 'lxc'), b0)], writes=[('ps', pi)])
                        sch.op('act', lambda pa=pa, b0=b0, nn=nn: nc.scalar.activation(
                            out=A[:, b0:b0 + nn], in_=ps[pa][:, :nn], func=AF.Sigmoid,
                            bias=sm[:, oa + d * 4 + q:oa + d * 4 + q + 1], scale=1.0),
                            reads=[('ps', pa), 'sm'], writes=[(K(bs, 'A'), b0)])
                        sch.op('act', lambda pi=pi, b0=b0, nn=nn: nc.scalar.activation(
                            out=Bt[:, b0:b0 + nn], in_=ps[pi][:, :nn], func=AF.Sigmoid,
                            bias=sm[:, oi + d * 4 + q:oi + d * 4 + q + 1], scale=1.0),
                            reads=[('ps', pi), 'sm'], writes=[(K(bs, 'B'), b0)])

                def coeffs2(bs, q, d, n, lk):
                    lxc, A, Bt, tmp = bs['lxc'], bs['A'], bs['B'], bs['tmp']
                    allA = [(K(bs, 'A'), b0) for b0 in range(0, n, NB)]
                    allB = [(K(bs, 'B'), b0) for b0 in range(0, n, NB)]
                    tk = [K(bs, 'tmp')]
                    sch.op('dve', lambda: nc.vector.tensor_tensor(out=Bt[:, :n], in0=Bt[:, :n], in1=lxc[:, :n], op=ALU.mult),
                           reads=allB + lk, writes=allB)
                    sch.op('act', lambda: nc.scalar.activation(out=tmp[:, :n], in_=A[:, :n], func=AF.Exp,
                                                               scale=cA[:, 1, d * 4 + q:d * 4 + q + 1]),
                           reads=allA + ['lr_cA'], writes=tk)
                    sch.op('act', lambda: nc.scalar.activation(out=A[:, :n], in_=A[:, :n], func=AF.Exp,
                                                               scale=cA[:, 0, d * 4 + q:d * 4 + q + 1]),
                           reads=allA + ['lr_cA'], writes=allA)
                    sch.op('act', lambda: nc.scalar.activation(out=tmp[:, :n], in_=tmp[:, :n], func=AF.Sqrt, scale=-1.0, bias=1.0),
                           reads=tk, writes=tk)
                    sch.op('dve', lambda: nc.vector.tensor_tensor(out=Bt[:, :n], in0=Bt[:, :n], in1=tmp[:, :n], op=ALU.mult),
                           reads=allB + tk, writes=allB)
                    return allA, allB

                def scan(bs, d, n, out_ap, outk, init, initk, allA, allB):
                    A, Bt = bs['A'], bs['B']
                    if d == 0:
                        f = lambda: nc.vector.tensor_tensor_scan(out=out_ap, data0=A[:, :n], data1=Bt[:, :n], initial=init,
                                                                 op0=ALU.mult, op1=ALU.add)
                    else:
                        f = lambda: nc.vector.tensor_tensor_scan(out=out_ap[:, ::-1], data0=A[:, :n][:, ::-1],
                                                                 data1=Bt[:, :n][:, ::-1], initial=init, op0=ALU.mult, op1=ALU.add)
                    sch.op('dve', f, reads=allA + allB + initk, writes=outk)

                def nextbs():
                    b = BS[cnt[0] % 3]
                    cnt[0] += 1
                    return b

                def pipeline(units):
                    for k in range(len(units)):
                        if k == 0:
                            units[0][0]()
                        if k + 1 < len(units):
                            units[k + 1][0]()
                        units[k][1]()

                for q in range(4):
                    sch.op('pool', lambda: nc.gpsimd.memset(wblk[:], 0.0), writes=['lr_w'])
                    for d in range(2):
                        for g, wsrc in ((0, lru_w_a), (1, lru_w_i)):
                            for hh in range(2):
                                sch.dma('sp', wblk[hh * 64:(hh + 1) * 64, d, g, hh * 64:(hh + 1) * 64], wsrc[d, 2 * q + hh],
                                        reads=[], writes=['lr_w'])
                    for j in range(4):
                        sch.op('dve', lambda j=j: nc.vector.tensor_scalar(
                            out=dg[:, j, :], in0=identb[:], scalar1=sm[:, ocw + q * 4 + j:ocw + q * 4 + j + 1], scalar2=None, op0=ALU.mult),
                            reads=['lr_id', 'sm'], writes=['lr_dg'])
                    row_l = 1536 + q * 128
                    row_g = 2048 + q * 128
                    units = []
                    state = {}

                    def mk_unit(d, kind, sg, first_of_dir, last_of_dir):
                        bs = nextbs()
                        n = C if kind == 'c' else T
                        st = {}

                        def stageA():
                            raw = bs['raw']
                            rk = K(bs, 'raw')
                            if kind == 'c':
                                sch.op('pool', lambda: nc.gpsimd.memset(raw[:, 0:2], 0.0), writes=[rk])
                                sch.op('pool', lambda: nc.gpsimd.memset(raw[:, 2 + C:3 + C], 0.0), writes=[rk])
                                sch.dma('sp', raw[:, 2:2 + C], pc[row_l:row_l + 128, :], writes=[rk])
                            else:
                                t0 = sg * T
                                lo = max(t0 - 2, 0)
                                hi = min(t0 + T + 1, S)
                                if lo > t0 - 2:
                                    sch.op('pool', lambda: nc.gpsimd.memset(raw[:, 0:2], 0.0), writes=[rk])
                                if hi < t0 + T + 1:
                                    sch.op('pool', lambda: nc.gpsimd.memset(raw[:, T + 2:T + 3], 0.0), writes=[rk])
                                d0 = lo - (t0 - 2)
                                sch.dma('sp', raw[:, d0:d0 + hi - lo], px[row_l:row_l + 128, lo:hi], writes=[rk])
                            lk = conv4(bs, q, n)
                            st['lk'] = lk
                            st['gk'] = gates(bs, q, d, n)

                        def stageB():
                            allA, allB = coeffs2(bs, q, d, n, st['lk'])
                            if kind == 'c':
                                scan(bs, d, C, hcs[:, d, :], [('lr_hcs', d)], 0.0, [], allA, allB)
                                state['init'] = hcs[:, d, C - 1:C] if d == 0 else hcs[:, d, 0:1]
                                state['initk'] = [('lr_hcs', d)]
                                return
                            t0 = sg * T
                            init, initk = state['init'], state['initk']
                            if d == 0:
                                scan(bs, d, T, rx[:, t0:t0 + T], [('lr_rx', sg)], init, initk, allA, allB)
                                if not last_of_dir:
                                    sch.op('dve', lambda: nc.vector.tensor_copy(out=carry[:, 0:1], in_=rx[:, t0 + T - 1:t0 + T]),
                                           reads=[('lr_rx', sg)] + initk, writes=[('lr_carry', 0)])
                                    state['init'], state['initk'] = carry[:, 0:1], [('lr_carry', 0)]
                            else:
                                tmp = bs['tmp']
                                tk = [K(bs, 'tmp')]
                                scan(bs, d, T, tmp[:, :T], tk, init, initk, allA, allB)
                                if not last_of_dir:
                                    sch.op('dve', lambda: nc.vector.tensor_copy(out=carry[:, 1:2], in_=tmp[:, 0:1]),
                                           reads=tk + initk, writes=[('lr_carry', 1)])
                                    state['init'], state['initk'] = carry[:, 1:2], [('lr_carry', 1)]
                                sch.op('dve', lambda: nc.vector.tensor_tensor(out=rx[:, t0:t0 + T], in0=rx[:, t0:t0 + T],
                                                                              in1=tmp[:, :T], op=ALU.add),
                                       reads=tk + [('lr_rx', sg)], writes=[('lr_rx', sg)])
                        return (stageA, stageB)

                    for d in range(2):
                        units.append(mk_unit(d, 'c', None, True, False))
                        segs = list(range(NSEG)) if d == 0 else list(range(NSEG - 1, -1, -1))
                        for si, sg in enumerate(segs):
                            units.append(mk_unit(d, 'x', sg, False, si == NSEG - 1))

                    def mk_gate(kind, sg):
                        bs = nextbs()
                        n = C if kind == 'c' else T
                        gxb, A, Bt, tmp, ob = bs['gx'], bs['A'], bs['B'], bs['tmp'], bs['ob']
                        ak = [(K(bs, 'A'), b0) for b0 in range(0, n, NB)]
                        tk = [K(bs, 'tmp')]
                        gk = [K(bs, 'gx')]

                        def stageA():
                            if kind == 'c':
                                sch.dma('sp', gxb[:, :C], pc[row_g:row_g + 128, :], writes=gk)
                            else:
                                sch.dma('sp', gxb[:, :T], px[row_g:row_g + 128, sg * T:(sg + 1) * T], writes=gk)
                            src_ = gxb[:, :n]
                            sch.op('act', lambda: nc.scalar.activation(out=A[:, :n], in_=src_, func=AF.Square), reads=gk, writes=ak)
                            sch.op('dve', lambda: nc.vector.tensor_scalar(out=A[:, :n], in0=A[:, :n], scalar1=0.044715, scalar2=1.0,
                                                                          op0=ALU.mult, op1=ALU.add), reads=ak, writes=ak)
                            sch.op('dve', lambda: nc.vector.tensor_tensor(out=A[:, :n], in0=A[:, :n], in1=src_, op=ALU.mult),
                                   reads=ak + gk, writes=ak)

                        def stageB():
                            src_ = gxb[:, :n]
                            sch.op('act', lambda: nc.scalar.activation(out=A[:, :n], in_=A[:, :n], func=AF.Sigmoid, scale=1.5957691216057308),
                                   reads=ak, writes=ak)
                            sch.op('dve', lambda: nc.vector.tensor_tensor(out=tmp[:, :n], in0=A[:, :n], in1=src_, op=ALU.mult),
                                   reads=ak + gk, writes=tk)
                            if kind == 'c':
                                sch.op('dve', lambda: nc.vector.tensor_tensor(out=Bt[:, :C], in0=hcs[:, 0, :], in1=hcs[:, 1, :], op=ALU.add),
                                       reads=[('lr_hcs', 0), ('lr_hcs', 1)], writes=[(K(bs, 'B'), 0)])
                                sch.op('dve', lambda: nc.vector.tensor_tensor(out=ob[:, :C], in0=Bt[:, :C], in1=tmp[:, :C], op=ALU.mult),
                                       reads=tk + [(K(bs, 'B'), 0)], writes=[K(bs, 'ob')])
                                sch.dma('pool', ymixc[512 + q * 128:512 + (q + 1) * 128, :], ob[:, :C], reads=[K(bs, 'ob')])
                            else:
                                t0 = sg * T
                                sch.op('dve', lambda: nc.vector.tensor_tensor(out=ob[:, :T], in0=rx[:, t0:t0 + T], in1=tmp[:, :T], op=ALU.mult),
                                       reads=tk + [('lr_rx', sg)], writes=[K(bs, 'ob')])
                                sch.dma('pool', ymix[512 + q * 128:512 + (q + 1) * 128, t0:t0 + T], ob[:, :T], reads=[K(bs, 'ob')])
                        return (stageA, stageB)

                    for sg in range(NSEG):
                        units.append(mk_gate('x', sg))
                    units.append(mk_gate('c', None))
                    pipeline(units)
                sch.barrier()

        def hyena_all(L, pre, src, dst):
            P = hy_params(L)
            N, N2, NH, NK1, NGr, FB, NBLK = P['N'], P['N2'], P['NH'], P['NK1'], P['NG'], P['FB'], P['NBLK']
            W3 = NK1 * 3
            f1tab_d = kb.inp(pre + "f1tab", [N2, W3], BF16)
            gtab_d = kb.inp(pre + "gtab", [128, 2, NK1, 128], BF16)
            gttab_d = kb.inp(pre + "gttab", [128, 2, NK1, 128], BF16)
            etab_d = kb.inp(pre + "etab", [NK1, 2, NH], BF16)
            zfeat_d = kb.inp(pre + "zfeat", [33, N])
            ndelta_d = kb.inp(pre + "ndelta", [128, 4])
            lagmin_d = kb.inp(pre + "lagmin", [128, NBLK])
            iota_d = kb.inp(pre + "iota", [128, FB])
            kt = kb.scratch(pre + "kt", [2, 512, N], BF16)
            kf = kb.scratch(pre + "kf", [2, 4, NGr, 128, 4 * 2 * 128], BF16)
            uc = kb.scratch(pre + "uc", [1536, L], BF16)
            z1 = kb.scratch(pre + "z1", [512, L], BF16)
            dd = kb.scratch(pre + "dd", [NK1, 2 * 128 * 128], BF16)
            TWO_PI = 2.0 * math.pi

            with ExitStack() as hp:
                hnrm = kb.sb(hp, pre + "hnrm", [128, 8], F32)
                hsc = kb.sb(hp, pre + "hsc", [128, 2, 8], F32)
                def fft_fwd(ph, rows_ap, nrow_k, epilogue):
                    Xs = kb.sb(ph, "Xs", [nrow_k, 128, 128], BF16)
                    Bp = kb.sb(ph, "Bp", [128, 128, W3], BF16)
                    v = rows_ap.rearrange("c (a b) -> a c b", b=128)
                    for c0 in range(0, 128, 32):
                        sch.dma('sp', Xs[:, c0:c0 + 32, :], v[:, c0:c0 + 32, :], writes=[('Xs', c0)])
                    cpb = 2 if W3 > 128 else 32
                    slot = 512 // cpb
                    ngrp = 128 // cpb
                    bpk = [('Bp', cp) for cp in range(ngrp)]
                    for cp in range(ngrp):
                        pb = nextps()
                        for cc in range(cpb):
                            c = cp * cpb + cc
                            sch.op('pe', lambda c=c, cc=cc, pb=pb: nc.tensor.matmul(
                                out=ps[pb][:, cc * slot:cc * slot + W3], lhsT=Xs[:, c, :], rhs=f1tab[0:nrow_k, :],
                                start=True, stop=True),
                                reads=[('Xs', (c // 32) * 32), 'f1tab'], writes=[('ps', pb)])
                        src_ap = ps[pb][:, :].rearrange("p (c w) -> p c w", c=cpb)[:, :, 0:W3]
                        dst_ap = Bp[:, cp * cpb:(cp + 1) * cpb, :]
                        if cp % 2 == 0:
                            sch.op('dve', lambda s_=src_ap, d_=dst_ap: nc.vector.tensor_copy(out=d_, in_=s_),
                                   reads=[('ps', pb)], writes=[('Bp', cp)])
                        else:
                            sch.op('act', lambda s_=src_ap, d_=dst_ap: nc.scalar.copy(out=d_, in_=s_),
                                   reads=[('ps', pb)], writes=[('Bp', cp)])
                    pending = [None]
                    for g in range(NGr):
                        ba = nextps()
                        bb = nextps()
                        for kl in range(4):
                            k1 = g * 4 + kl
                            pb = ba if kl < 2 else bb
                            cs = slice((kl % 2) * 256, (kl % 2) * 256 + 256)
                            for (lt, c0, st_) in ((0, 1, True), (1, 0, False)):
                                sch.op('pe', lambda pb=pb, lt=lt, c0=c0, st_=st_, k1=k1, cs=cs: nc.tensor.matmul(
                                    out=ps[pb][:, cs], lhsT=gtab[:, lt, k1, :],
                                    rhs=Bp[:, :, k1 * 3 + c0:k1 * 3 + c0 + 2].rearrange("p c m -> p m c"),
                                    start=st_, stop=not st_),
                                    reads=bpk + ['gtab'], writes=[('ps', pb)])
                        nb_ = epilogue(g, ba, bb)
                        if pending[0] is not None:
                            pending[0]()
                        pending[0] = nb_
                    if pending[0] is not None:
                        pending[0]()

                with ExitStack() as ph:
                    h2all = kb.sb(ph, "h2all", [64, N], F32)
                    w1s = kb.sb(ph, "w1s", [33, 64], F32)
                    w2s = kb.sb(ph, "w2s", [64, 64], F32)
                    w3s = kb.sb(ph, "w3s", [64, 2048], F32)
                    bfq = kb.sb(ph, "bfq", [64, 2], F32)
                    sch.dma('sp', w1s[:], hy_f_w1, writes=['w1s'])
                    sch.dma('sp', w2s[:], hy_f_w2, writes=['w2s'])
                    sch.dma('sp', w3s[:], hy_f_w3, writes=['w3s'])
                    fq = smc('hyf_freq')[0:64]
                    sch.op('dve', lambda: nc.vector.tensor_tensor(out=bfq[:, 0:1], in0=smc('hyf_b1')[0:64], in1=fq, op=ALU.mult),
                           reads=['sm'], writes=['bfq'])
                    sch.op('dve', lambda: nc.vector.tensor_tensor(out=bfq[:, 1:2], in0=smc('hyf_b2')[0:64], in1=fq, op=ALU.mult),
                           reads=['sm'], writes=['bfq'])

                    def sin_layer(pb, n, bcol, arg, argk, tt, ttk, out_ap, outk):
                        sch.op('act', lambda: nc.scalar.activation(out=arg[:, :n], in_=ps[pb][0:64, :n], func=AF.Identity,
                                                                   scale=fq, bias=bfq[:, bcol:bcol + 1]),
                               reads=[('ps', pb), 'bfq', 'sm'], writes=[argk])
                        for (cmp_, thr, sgn) in ((ALU.is_gt, math.pi, ALU.subtract), (ALU.is_lt, -math.pi, ALU.add)):
                            sch.op('dve', lambda cmp_=cmp_, thr=thr: nc.vector.tensor_scalar(
                                out=tt[:, :n], in0=arg[:, :n], scalar1=thr, scalar2=TWO_PI, op0=cmp_, op1=ALU.mult),
                                reads=[argk], writes=[ttk])
                            sch.op('dve', lambda sgn=sgn: nc.vector.tensor_tensor(out=arg[:, :n], in0=arg[:, :n], in1=tt[:, :n], op=sgn),
                                   reads=[argk, ttk], writes=[argk])
                        sch.op('act', lambda: nc.scalar.activation(out=out_ap, in_=arg[:, :n], func=AF.Sin),
                               reads=[argk], writes=[outk])

                    with ExitStack() as p1:
                        zb = [kb.sb(p1, "zb%d" % i, [33, FB], F32) for i in range(2)]
                        arg = kb.sb(p1, "arg", [64, FB], F32)
                        tt = kb.sb(p1, "tt", [64, FB], F32)
                        h1 = kb.sb(p1, "h1", [64, FB], F32)
                        for j in range(NBLK):
                            z_ = zb[j % 2]
                            zk = 'zb%d' % (j % 2)
                            sch.dma('sp', z_[:], zfeat_d[:, j * FB:(j + 1) * FB], writes=[zk])
                            pb = nextps()
                            sch.op('pe', lambda z_=z_, pb=pb: nc.tensor.matmul(out=ps[pb][0:64, :FB], lhsT=w1s[:], rhs=z_[:],
                                                                              start=True, stop=True),
                                   reads=[zk, 'w1s'], writes=[('ps', pb)])
                            sin_layer(pb, FB, 0, arg, 'arg', tt, 'tt', h1[:, :FB], 'h1')
                            pb = nextps()
                            sch.op('pe', lambda pb=pb: nc.tensor.matmul(out=ps[pb][0:64, :FB], lhsT=w2s[:], rhs=h1[:, :FB],
                                                                        start=True, stop=True),
                                   reads=['h1', 'w2s'], writes=[('ps', pb)])
                            sin_layer(pb, FB, 1, arg, 'arg', tt, 'tt', h2all[:, j * FB:(j + 1) * FB], ('h2', j))
                        sch.barrier()
                    with ExitStack() as p2:
                        ktb2 = kb.sb(p2, "ktb2", [128, 2, N], BF16)
                        wb = kb.sb(p2, "wb", [128, FB], F32)
                        scj = kb.sb(p2, "scj", [128, NBLK], F32)
                        wt = [kb.sb(p2, "wt%d" % i, [128, FB], F32) for i in range(2)]
                        junk = [kb.sb(p2, "junk%d" % i, [128, FB], BF16) for i in range(2)]
                        acc = kb.sb(p2, "acc", [128, 2, NBLK], F32)
                        ndl = kb.sb(p2, "ndl", [128, 4], F32)
                        lgm = kb.sb(p2, "lgm", [128, NBLK], F32)
                        iot = kb.sb(p2, "iot", [128, FB], F32)
                        sch.dma('sp', ndl[:], ndelta_d, writes=['ndl'])
                        sch.dma('sp', lgm[:], lagmin_d, writes=['lgm'])
                        sch.dma('sp', iot[:], iota_d, writes=['iot'])
                        wc = 0
                        jc = 0
                        for q in range(4):
                            sch.op('act', lambda q=q: nc.scalar.activation(out=wb[:], in_=iot[:], func=AF.Exp, scale=ndl[:, q:q + 1]),
                                   reads=['iot', 'ndl'], writes=['wb'])
                            sch.op('act', lambda q=q: nc.scalar.activation(out=scj[:], in_=lgm[:], func=AF.Exp, scale=ndl[:, q:q + 1]),
                                   reads=['lgm', 'ndl'], writes=['scj'])
                            sch.op('dve', lambda: nc.vector.memset(acc[:], 0.0), writes=['acc'])
                            for j in range(NBLK):
                                hf = 1 if j * FB >= L else 0
                                w_ = wt[wc % 2]
                                wk_ = 'wt%d' % (wc % 2)
                                wc += 1
                                wsrc = wb[:, ::-1] if hf else wb[:, :]
                                sch.op('dve', lambda w_=w_, wsrc=wsrc, j=j: nc.vector.tensor_scalar(
                                    out=w_[:], in0=wsrc, scalar1=scj[:, j:j + 1], scalar2=0.05, op0=ALU.mult, op1=ALU.add),
                                    reads=['wb', 'scj'], writes=[wk_])
                                for n in range(2):
                                    col = hf * 1024 + n * 512 + q * 128
                                    pb = nextps()
                                    sch.op('pe', lambda pb=pb, col=col, j=j: nc.tensor.matmul(
                                        out=ps[pb][:, :FB], lhsT=w3s[:, col:col + 128], rhs=h2all[:, j * FB:(j + 1) * FB],
                                        start=True, stop=True), reads=['w3s', ('h2', j)], writes=[('ps', pb)])
                                    kk_ = ('ktb2', n, j)
                                    sch.op('dve', lambda w_=w_, pb=pb, j=j, n=n: nc.vector.tensor_tensor(
                                        out=ktb2[:, n, j * FB:(j + 1) * FB], in0=ps[pb][:, :FB], in1=w_[:], op=ALU.mult),
                                        reads=[wk_, ('ps', pb)], writes=[kk_])
                                    if j * FB == L:
                                        sch.op('dve', lambda n=n: nc.vector.memset(ktb2[:, n, L:L + 1], 0.0), reads=[kk_], writes=[kk_])
                                    jk = junk[jc % 2]
                                    jkk = 'junk%d' % (jc % 2)
                                    jc += 1
                                    sch.op('act', lambda jk=jk, n=n, j=j: nc.scalar.activation(
                                        out=jk[:], in_=ktb2[:, n, j * FB:(j + 1) * FB], func=AF.Abs, accum_out=acc[:, n, j:j + 1]),
                                        reads=[kk_, 'acc'], writes=[jkk, ('acc', n, j)])
                            for n in range(2):
                                sch.op('dve', lambda n=n, q=q: nc.vector.tensor_reduce(
                                    out=hnrm[:, n * 4 + q:n * 4 + q + 1], in_=acc[:, n, :], axis=mybir.AxisListType.X, op=ALU.add),
                                    reads=[('acc', n, j) for j in range(NBLK)] + ['acc'], writes=[('hnrm', n, q)])
                                sch.dma('pool', kt[n, q * 128:(q + 1) * 128, :], ktb2[:, n, :], reads=[('ktb2', n, j) for j in range(NBLK)])
                        sch.barrier()
                sch.barrier()

                f1tab = kb.sb(hp, pre + "f1tab", [N2, W3], BF16)
                gtab = kb.sb(hp, pre + "gtab", [128, 2, NK1, 128], BF16)
                gttab = kb.sb(hp, pre + "gttab", [128, 2, NK1, 128], BF16)
                etab = kb.sb(hp, pre + "etab", [NK1, 2, NH], BF16)
                sch.dma('sp', f1tab[:], f1tab_d, writes=['f1tab'])
                for r in range(2):
                    sch.dma('sp', gtab[:, r], gtab_d[:, r], writes=['gtab'])
                    sch.dma('sp', gttab[:, r], gttab_d[:, r], writes=['gttab'])
                sch.dma('sp', etab[:], etab_d, writes=['etab'])

                for n in range(2):
                    for q in range(4):
                        with ExitStack() as ph:
                            ub = [kb.sb(ph, "ub%d" % i, [128, 4, 2, 128], BF16) for i in range(2)]

                            def ep_store(g, ba, bb, n=n, q=q, ub=ub):
                                u_ = ub[g % 2]
                                uk = 'ub%d' % (g % 2)
                                sch.op('dve', lambda: nc.vector.tensor_copy(
                                    out=u_[:, 0:2, :, :].rearrange("p k r c -> p (k r c)"), in_=ps[ba][:, :]),
                                    reads=[('ps', ba)], writes=[(uk, 0)])
                                sch.op('act', lambda: nc.scalar.copy(
                                    out=u_[:, 2:4, :, :].rearrange("p k r c -> p (k r c)"), in_=ps[bb][:, :]),
                                    reads=[('ps', bb)], writes=[(uk, 1)])
                                sch.dma('pool', kf[n, q, g], u_[:].rearrange("p k r c -> p (k r c)"), reads=[(uk, 0), (uk, 1)])
                            fft_fwd(ph, kt[n, q * 128:(q + 1) * 128, :], N2, ep_store)
                            sch.barrier()

                with ExitStack() as ph:
                    T = min(L, 2048)
                    raws = [kb.sb(ph, "hraw%d" % i, [128, T + 2], BF16) for i in range(2)]
                    obs = [kb.sb(ph, "hob%d" % i, [128, T], BF16) for i in range(2)]
                    idb = kb.sb(ph, "hidb", [128, 128], BF16)
                    dg3 = [kb.sb(ph, "hdg%d" % i, [128, 3, 128], BF16) for i in range(2)]
                    sch.dma('sp', idb[:], ident_d, writes=['hidb'])
                    ocw, _ = off['hy_cw']
                    ocb, _ = off['hy_cb']
                    it = 0
                    for qq in range(12):
                        dg = dg3[qq % 2]
                        dgk = 'hdg%d' % (qq % 2)
                        for j in range(3):
                            sch.op('pool', lambda j=j, dg=dg, qq=qq: nc.gpsimd.tensor_scalar(
                                out=dg[:, j, :], in0=idb[:], scalar1=sm[:, ocw + qq * 3 + j:ocw + qq * 3 + j + 1], scalar2=None, op0=ALU.mult),
                                reads=['hidb', 'sm'], writes=[dgk])
                        for t0 in range(0, L, T):
                            raw = raws[it % 2]
                            rk = 'hraw%d' % (it % 2)
                            ob = obs[it % 2]
                            okk = 'hob%d' % (it % 2)
                            it += 1
                            lo = max(t0 - 1, 0)
                            hi = min(t0 + T + 1, L)
                            if t0 == 0:
                                sch.op('pool', lambda raw=raw: nc.gpsimd.memset(raw[:, 0:1], 0.0), writes=[rk])
                            if t0 + T >= L:
                                sch.op('pool', lambda raw=raw: nc.gpsimd.memset(raw[:, T + 1:T + 2], 0.0), writes=[rk])
                            d0 = lo - (t0 - 1)
                            sch.dma('sp', raw[:, d0:d0 + hi - lo], src[qq * 128:(qq + 1) * 128, lo:hi], writes=[rk])
                            for bi, b0 in enumerate(range(0, T, NB)):
                                nn = min(NB, T - b0)
                                pb = nextps()
                                for j in range(3):
                                    sch.op('pe', lambda pb=pb, j=j, b0=b0, nn=nn, raw=raw, dg=dg: nc.tensor.matmul(
                                        out=ps[pb][:, :nn], lhsT=dg[:, j, :], rhs=raw[:, j + b0:j + b0 + nn], start=(j == 0), stop=(j == 2)),
                                        reads=[rk, dgk], writes=[('ps', pb)])
                                if bi % 2 == 0:
                                    sch.op('act', lambda pb=pb, b0=b0, nn=nn, ob=ob, qq=qq: nc.scalar.activation(
                                        out=ob[:, b0:b0 + nn], in_=ps[pb][:, :nn], func=AF.Identity, bias=sm[:, ocb + qq:ocb + qq + 1], scale=1.0),
                                        reads=[('ps', pb), 'sm'], writes=[(okk, b0)])
                                else:
                                    sch.op('dve', lambda pb=pb, b0=b0, nn=nn, ob=ob, qq=qq: nc.vector.tensor_scalar(
                                        out=ob[:, b0:b0 + nn], in0=ps[pb][:, :nn], scalar1=sm[:, ocb + qq:ocb + qq + 1], scalar2=None, op0=ALU.add),
                                        reads=[('ps', pb), 'sm'], writes=[(okk, b0)])
                            sch.dma('pool', uc[qq * 128:(qq + 1) * 128, t0:t0 + T], ob[:, :T], reads=[(okk, b0) for b0 in range(0, T, NB)])
                    sch.barrier()

                obias, _ = off['hy_bias']
                hk = [('hnrm', n_, q_) for n_ in range(2) for q_ in range(4)]
                sch.op('dve', lambda: nc.vector.reciprocal(out=hsc[:, 0, :], in_=hnrm[:]), reads=hk, writes=['hsc'])
                sch.op('dve', lambda: nc.vector.tensor_tensor(out=hsc[:, 1, :], in0=hnrm[:], in1=sm[:, obias:obias + 8], op=ALU.mult),
                       reads=hk + ['sm'], writes=['hsc'])
                for n in range(2):
                    zin = uc[0:512, :] if n == 0 else z1
                    zout = z1 if n == 0 else dst
                    for q in range(4):
                        rows = zin[q * 128:(q + 1) * 128, :]
                        with ExitStack() as ph:
                            kfg = [kb.sb(ph, "kfg%d" % i, [128, 4, 2, 128], BF16) for i in range(2)]
                            Yt = [kb.sb(ph, "Yt%d" % i, [128, 4, 3, 128], BF16) for i in range(2)]
                            tq = [kb.sb(ph, "tq%d" % i, [128, 4, 2, 128], F32) for i in range(4)]
                            dst_ = [kb.sb(ph, "dst%d" % i, [128, 4, 2, 128], BF16) for i in range(2)]

                            def ep_conv(g, ba, bb, n=n, q=q, kfg=kfg, Yt=Yt, tq=tq, dst_=dst_):
                                kg = kfg[g % 2]
                                kk = 'kfg%d' % (g % 2)
                                Y = Yt[g % 2]
                                yk = 'Yt%d' % (g % 2)
                                P1 = tq[(g % 2) * 2]
                                P2 = tq[(g % 2) * 2 + 1]
                                p1k = ('tq', (g % 2) * 2)
                                p2k = ('tq', (g % 2) * 2 + 1)
                                sch.dma('sp', kg[:].rearrange("p k r c -> p (k r c)"), kf[n, q, g], writes=[kk])
                                for hb, pb in enumerate((ba, bb)):
                                    uv = ps[pb][:, :].rearrange("p (k r c) -> p k r c", k=2, r=2)
                                    ks = slice(hb * 2, hb * 2 + 2)
                                    sch.op('dve', lambda pb=pb, ks=ks: nc.vector.tensor_tensor(
                                        out=P1[:, ks].rearrange("p k r c -> p (k r c)"), in0=ps[pb][:, :],
                                        in1=kg[:, ks].rearrange("p k r c -> p (k r c)"), op=ALU.mult),
                                        reads=[('ps', pb), kk], writes=[(p1k, hb)])
                                    for r in range(2):
                                        sch.op('dve', lambda uv=uv, ks=ks, r=r: nc.vector.tensor_tensor(
                                            out=P2[:, ks, r, :], in0=uv[:, :, r, :], in1=kg[:, ks, 1 - r, :], op=ALU.mult),
                                            reads=[('ps', pb), kk], writes=[(p2k, hb, r)])
                                p1r = [(p1k, 0), (p1k, 1)]
                                p2r = [(p2k, hb, r) for hb in range(2) for r in range(2)]
                                sch.op('pool', lambda: nc.gpsimd.tensor_tensor(out=Y[:, :, 1, :], in0=P1[:, :, 0, :], in1=P1[:, :, 1, :], op=ALU.subtract),
                                       reads=p1r, writes=[(yk, 1)])
                                sch.op('pool', lambda: nc.gpsimd.tensor_tensor(out=Y[:, :, 2, :], in0=P2[:, :, 0, :], in1=P2[:, :, 1, :], op=ALU.add),
                                       reads=p2r, writes=[(yk, 2)])
                                sch.op('act', lambda: nc.scalar.mul(out=Y[:, :, 0, :], in_=Y[:, :, 2, :], mul=-1.0), reads=[(yk, 2)], writes=[(yk, 0)])

                                def part_b(g=g, Y=Y, yk=yk):
                                    da = nextps()
                                    db = nextps()
                                    for kl in range(4):
                                        k1 = g * 4 + kl
                                        pb = da if kl < 2 else db
                                        cs = slice((kl % 2) * 256, (kl % 2) * 256 + 256)
                                        for (lt, c0, st_) in ((0, 1, True), (1, 0, False)):
                                            sch.op('pe', lambda pb=pb, lt=lt, c0=c0, st_=st_, k1=k1, cs=cs, kl=kl: nc.tensor.matmul(
                                                out=ps[pb][:, cs], lhsT=gttab[:, lt, k1, :],
                                                rhs=Y[:, kl, c0:c0 + 2, :].rearrange("p m c -> p (m c)"), start=st_, stop=not st_),
                                                reads=[(yk, 0), (yk, 1), (yk, 2), 'gttab'], writes=[('ps', pb)])
                                    d_ = dst_[g % 2]
                                    dk = 'dst%d' % (g % 2)
                                    sch.op('act', lambda: nc.scalar.copy(out=d_[:, 0:2].rearrange("p k r c -> p (k r c)"), in_=ps[da][:, :]),
                                           reads=[('ps', da)], writes=[(dk, 0)])
                                    sch.op('dve', lambda: nc.vector.tensor_copy(out=d_[:, 2:4].rearrange("p k r c -> p (k r c)"), in_=ps[db][:, :]),
                                           reads=[('ps', db)], writes=[(dk, 1)])
                                    sch.dma('act', dd[g * 4:(g + 1) * 4, :].rearrange("k (r n c) -> n k r c", r=2, n=128),
                                            d_[:], reads=[(dk, 0), (dk, 1)])
                                return part_b
                            fft_fwd(ph, rows, NH, ep_conv)
                            sch.barrier()
                        with ExitStack() as ph:
                            Dl = kb.sb(ph, "Dl", [NK1, 2, 128, 128], BF16)
                            zc = kb.sb(ph, "zc", [128, L], BF16)
                            xg = kb.sb(ph, "xg", [128, L], BF16)
                            zo = kb.sb(ph, "zo", [128, L], BF16)
                            tf = [kb.sb(ph, "tf%d" % i, [128, 512], F32) for i in range(2)]
                            ddv = dd[:, :].rearrange("k (r n c) -> k r n c", r=2, n=128)
                            for hh in range(2):
                                for r in range(2):
                                    for p0 in range(0, NK1, 16):
                                        p1 = min(NK1, p0 + 16)
                                        sch.dma('sp', Dl[p0:p1, r, hh * 64:(hh + 1) * 64, :], ddv[p0:p1, r, hh * 64:(hh + 1) * 64, :],
                                                writes=[('Dl', r, hh, p0)])
                            sch.dma('sp', zc[:], rows, writes=['zc'])
                            grow = 512 + n * 512 + q * 128
                            sch.dma('sp', xg[:], uc[grow:grow + 128, :], writes=['xg'])
                            npb = min(128, 512 // NH)
                            zv = zc[:, :].rearrange("p (a b) -> p b a", b=128)
                            xv = xg[:, :].rearrange("p (a b) -> p b a", b=128)
                            ov = zo[:, :].rearrange("p (a b) -> p b a", b=128)
                            for gi, n1g in enumerate(range(0, 128, npb)):
                                pb = nextps()
                                for nl in range(npb):
                                    n1 = n1g + nl
                                    for r in range(2):
                                        sch.op('pe', lambda pb=pb, nl=nl, n1=n1, r=r: nc.tensor.matmul(
                                            out=ps[pb][:, nl * NH:(nl + 1) * NH], lhsT=Dl[:, r, n1, :], rhs=etab[:, r, :],
                                            start=(r == 0), stop=(r == 1)),
                                            reads=[('Dl', r, n1 // 64, p0) for p0 in range(0, NK1, 16)] + ['etab'], writes=[('ps', pb)])
                                t_ = tf[gi % 2]
                                tk = 'tf%d' % (gi % 2)
                                tv = t_[:, 0:npb * NH].rearrange("p (b a) -> p b a", a=NH)
                                pv = ps[pb][:, 0:npb * NH].rearrange("p (b a) -> p b a", a=NH)
                                sch.op('dve', lambda tv=tv, pv=pv, n1g=n1g: nc.vector.scalar_tensor_tensor(
                                    out=tv, in0=zv[:, n1g:n1g + npb, :], scalar=hsc[:, 1, n * 4 + q:n * 4 + q + 1],
                                    in1=pv, op0=ALU.mult, op1=ALU.add),
                                    reads=['zc', ('ps', pb), 'hsc'], writes=[tk])
                                tvT = t_[:, 0:npb * NH].rearrange("p (b a) -> p a b", a=NH)
                                xvT = xg[:, :].rearrange("p (a b) -> p a b", b=128)[:, :, n1g:n1g + npb]
                                ovT = zo[:, :].rearrange("p (a b) -> p a b", b=128)[:, :, n1g:n1g + npb]
                                sch.op('dve', lambda tvT=tvT, xvT=xvT, ovT=ovT: nc.vector.scalar_tensor_tensor(
                                    out=ovT, in0=tvT, scalar=hsc[:, 0, n * 4 + q:n * 4 + q + 1], in1=xvT, op0=ALU.mult, op1=ALU.mult),
                                    reads=[tk, 'xg', 'hsc'], writes=[('zo', gi)])
                            sch.dma('pool', zout[q * 128:(q + 1) * 128, :], zo[:], reads=[('zo', gi) for gi in range(128 // npb)])
                            sch.barrier()
                sch.barrier()


        def phase_qkrope(qkraw, qkraw_c, qr, kr, kcr):
            cos_d = kb.inp("rope_cos", [128, S])
            sin_d = kb.inp("rope_sin", [128, S])
            rmat_d = kb.inp("rope_R", [128, 128], BF16)
            bd_d = kb.inp("bd64", [128, 128], BF16)
            with ExitStack() as ph:
                cos_sb = kb.sb(ph, "cos", [128, S], F32)
                sin_sb = kb.sb(ph, "sin", [128, S], F32)
                rmat = kb.sb(ph, "rmat", [128, 128], BF16)
                bd = kb.sb(ph, "bd", [128, 128], BF16)
                raw = kb.sb(ph, "qraw", [128, S], BF16)
                sq = kb.sb(ph, "qsq", [128, S], BF16)
                rstd = kb.sb(ph, "qrstd", [128, S], F32)
                qn = kb.sb(ph, "qn", [128, S], BF16)
                ob = kb.sb(ph, "qob", [128, S], BF16)
                t1 = [kb.sb(ph, "qt1%d" % i, [128, NB], F32) for i in range(2)]
                t2 = [kb.sb(ph, "qt2%d" % i, [128, NB], F32) for i in range(2)]
                sch.dma('sp', cos_sb[:], cos_d, writes=['cos'])
                sch.dma('sp', sin_sb[:], sin_d, writes=['sin'])
                sch.dma('sp', rmat[:], rmat_d, writes=['rmat'])
                sch.dma('sp', bd[:], bd_d, writes=['bd'])
                tiles = [(qkraw[m * 128:(m + 1) * 128, :], S, 'qgain', True, qr[m * 128:(m + 1) * 128, :]) for m in range(8)]
                tiles += [(qkraw[1024 + P_ * 128:1024 + (P_ + 1) * 128, :], S, 'kgain', True, kr[P_ * 128:(P_ + 1) * 128, :]) for P_ in range(2)]
                tiles += [(qkraw_c[1024 + P_ * 128:1024 + (P_ + 1) * 128, :], C, 'kgain', False, kcr[P_ * 128:(P_ + 1) * 128, :]) for P_ in range(2)]
                cnt = 0
                for (srcr, T, gname, rope, dstr) in tiles:
                    nb = (T + NB - 1) // NB
                    sch.dma('sp', raw[:, :T], srcr, writes=['qraw'])
                    sch.op('act', lambda T=T: nc.scalar.activation(out=sq[:, :T], in_=raw[:, :T], func=AF.Square),
                           reads=['qraw'], writes=['qsq'])
                    rk = []
                    for b in range(nb):
                        n = min(NB, T - b * NB)
                        pb = nextps()
                        sch.op('pe', lambda pb=pb, b=b, n=n: nc.tensor.matmul(out=ps[pb][:, :n], lhsT=bd[:], rhs=sq[:, b * NB:b * NB + n],
                                                                         start=True, stop=True),
                               reads=['qsq', 'bd'], writes=[('ps', pb)])
                        sch.op('act', lambda pb=pb, b=b, n=n: nc.scalar.activation(out=rstd[:, b * NB:b * NB + n], in_=ps[pb][:, :n],
                                                                              func=AF.Ln, bias=epsc[:, 0:1], scale=1.0),
                               reads=[('ps', pb), 'epsc'], writes=[('qrstd', b)])
                        rk.append(('qrstd', b))
                    sch.op('act', lambda T=T: nc.scalar.activation(out=rstd[:, :T], in_=rstd[:, :T], func=AF.Exp, scale=-0.5),
                           reads=rk, writes=rk)
                    target = qn if rope else ob
                    tkeys = ['qn'] if rope else [('qob', b) for b in range(nb)]
                    sch.op('dve', lambda T=T, gname=gname, target=target: nc.vector.scalar_tensor_tensor(
                        out=target[:, :T], in0=raw[:, :T], scalar=smc(gname), in1=rstd[:, :T], op0=ALU.mult, op1=ALU.mult),
                        reads=rk + ['qraw', 'sm'], writes=tkeys)
                    if rope:
                        for b in range(nb):
                            cs = slice(b * NB, (b + 1) * NB)
                            pb = nextps()
                            sch.op('pe', lambda pb=pb, cs=cs: nc.tensor.matmul(out=ps[pb][:, :], lhsT=rmat[:], rhs=qn[:, cs], start=True, stop=True),
                                   reads=['qn', 'rmat'], writes=[('ps', pb)])
                            a1 = t1[cnt % 2]
                            a2 = t2[cnt % 2]
                            k1_ = 'qt1%d' % (cnt % 2)
                            k2_ = 'qt2%d' % (cnt % 2)
                            cnt += 1
                            sch.op('dve', lambda pb=pb, cs=cs, a1=a1: nc.vector.tensor_tensor(out=a1[:], in0=ps[pb][:, :], in1=sin_sb[:, cs], op=ALU.mult),
                                   reads=[('ps', pb), 'sin'], writes=[k1_])
                            sch.op('pool', lambda cs=cs, a2=a2: nc.gpsimd.tensor_tensor(out=a2[:], in0=qn[:, cs], in1=cos_sb[:, cs], op=ALU.mult),
                                   reads=['qn', 'cos'], writes=[k2_])
                            sch.op('dve', lambda cs=cs, a1=a1, a2=a2: nc.vector.tensor_tensor(out=ob[:, cs], in0=a1[:], in1=a2[:], op=ALU.add),
                                   reads=[k1_, k2_], writes=[('qob', b)])
                        sch.dma('act', dstr, ob[:, :T], reads=[('qob', b) for b in range(nb)])
                    else:
                        sch.dma('act', dstr, ob[:, :T], reads=[('qob', b) for b in range(nb)])
                sch.barrier()

        def phase_att(qr, kr, kcr, vtok, vctok, oT):
            mask_d = kb.inp("att_mask", [128, 2, 128], BF16)
            NQB = S // 128
            with ExitStack() as ph:
                kr_sb = kb.sb(ph, "kr_sb", [128, 2, S], BF16)
                kc_sb = kb.sb(ph, "kc_sb", [128, 2, C], BF16)
                vt = kb.sb(ph, "vt", [128, NQB, 260], BF16)
                vct = kb.sb(ph, "vct", [128, 2, 260], BF16)
                mask = kb.sb(ph, "mask", [128, 2, 128], BF16)
                ident = kb.sb(ph, "ident", [128, 128], BF16)
                esink = kb.sb(ph, "esink", [128, 16], F32)
                qb = [kb.sb(ph, "qb%d" % i, [128, 8, 128], BF16) for i in range(2)]
                Ptb = [kb.sb(ph, "Pt%d" % i, [128, 5, 512], BF16) for i in range(2)]
                obuf = kb.sb(ph, "obuf", [128, 16, 64], BF16)
                oTs = [kb.sb(ph, "oTs%d" % i, [128, 8, 128], BF16) for i in range(2)]
                den = kb.sb(ph, "den", [128, 4, 4], F32)
                for P_ in range(2):
                    sch.dma('sp', kr_sb[:, P_, :], kr[P_ * 128:(P_ + 1) * 128, :], writes=['kr_sb'])
                    sch.dma('sp', kc_sb[:, P_, :], kcr[P_ * 128:(P_ + 1) * 128, :], writes=['kc_sb'])
                vv = vtok.rearrange("(b p) w -> p b w", p=128)
                for b0 in range(0, NQB, 16):
                    sch.dma('sp', vt[:, b0:b0 + 16, :], vv[:, b0:b0 + 16, :], writes=['vt'])
                sch.dma('sp', vct[:], vctok.rearrange("(b p) w -> p b w", p=128), writes=['vct'])
                sch.dma('sp', mask[:], mask_d, writes=['mask'])
                sch.dma('sp', ident[:], ident_d, writes=['ident'])
                sch.op('act', lambda: nc.scalar.activation(out=esink[:], in_=smc('sink'), func=AF.Exp), reads=['sm'], writes=['esink'])
                sp_i = [0]

                def sbank():
                    b = sp_i[0]
                    sp_i[0] = (b + 1) % 4
                    return b
                for i in range(NQB):
                    q_ = qb[i % 2]
                    qk = 'qb%d' % (i % 2)
                    sch.dma('sp', q_[:], qr[:, i * 128:(i + 1) * 128].rearrange("(m p) t -> p m t", p=128), writes=[qk])
                    kbs = []
                    if i > 0:
                        kbs.append(('l', i - 1, 0))
                    kbs.append(('l', i, None))
                    if i < NQB - 1:
                        kbs.append(('l', i + 1, 1))
                    kbs += [('c', 0, None), ('c', 1, None)]
                    for g in range(4):
                        P_, half = g // 2, g % 2
                        rows = slice(half * 64, half * 64 + 64)
                        Pt = Ptb[g % 2]
                        ob_ = 4 + g
                        for idx, (kind, kb_, mi) in enumerate(kbs):
                            bank = sbank()
                            ksrc = kr_sb if kind == 'l' else kc_sb
                            kkey = 'kr_sb' if kind == 'l' else 'kc_sb'
                            sch.op('pe', lambda bank=bank, ksrc=ksrc, kb_=kb_, rows=rows, P_=P_: nc.tensor.matmul(
                                out=ps[bank][:, :], lhsT=ksrc[rows, P_, kb_ * 128:(kb_ + 1) * 128],
                                rhs=q_[rows, P_ * 4:(P_ + 1) * 4, :], start=True, stop=True),
                                reads=[kkey, qk], writes=[('ps', bank)])
                            sch.op('act', lambda bank=bank, idx=idx, Pt=Pt: nc.scalar.activation(
                                out=Pt[:, idx, :], in_=ps[bank][:, :], func=AF.Exp, scale=0.125),
                                reads=[('ps', bank)], writes=[('Pt', g % 2, idx)])
                            if mi is not None:
                                sch.op('dve', lambda idx=idx, mi=mi, Pt=Pt: nc.vector.tensor_tensor(
                                    out=Pt[:, idx, :].rearrange("p (j q) -> p j q", j=4),
                                    in0=Pt[:, idx, :].rearrange("p (j q) -> p j q", j=4),
                                    in1=mask[:, mi, :].unsqueeze(1).to_broadcast([128, 4, 128]), op=ALU.mult),
                                    reads=[('Pt', g % 2, idx), 'mask'], writes=[('Pt', g % 2, idx)])
                        for j in range(4):
                            for idx, (kind, kb_, mi) in enumerate(kbs):
                                vsrc = vt if kind == 'l' else vct
                                vkey = 'vt' if kind == 'l' else 'vct'
                                sch.op('pe', lambda j=j, idx=idx, vsrc=vsrc, kb_=kb_, ob_=ob_, Pt=Pt, g=g: nc.tensor.matmul(
                                    out=ps[ob_][:, j * 128:j * 128 + 65], lhsT=Pt[:, idx, j * 128:(j + 1) * 128],
                                    rhs=vsrc[:, kb_, g * 65:(g + 1) * 65], start=(idx == 0), stop=(idx == len(kbs) - 1)),
                                    reads=[('Pt', g % 2, idx), vkey], writes=[('ps', ob_)])
                        pv = ps[ob_][:, :].rearrange("p (j w) -> p j w", w=128)
                        sch.op('dve', lambda pv=pv, g=g: nc.vector.tensor_tensor(out=den[:, g, :], in0=pv[:, :, 64], in1=esink[:, g * 4:(g + 1) * 4], op=ALU.add),
                               reads=[('ps', ob_), 'esink'], writes=[('den', g)])
                        sch.op('dve', lambda g=g: nc.vector.reciprocal(out=den[:, g, :], in_=den[:, g, :]), reads=[('den', g)], writes=[('den', g)])
                        sch.op('dve', lambda pv=pv, g=g: nc.vector.tensor_tensor(
                            out=obuf[:, g * 4:(g + 1) * 4, :], in0=pv[:, :, 0:64],
                            in1=den[:, g, :].unsqueeze(2).to_broadcast([128, 4, 64]), op=ALU.mult),
                            reads=[('ps', ob_), ('den', g)], writes=[('obuf', g)])
                    bank = sbank()
                    pT = ps[bank][:, :].bitcast(BF16)
                    for m in range(8):
                        sch.op('pe', lambda m=m, pT=pT: nc.tensor.transpose(
                            out=pT[:, m * 128:(m + 1) * 128], in_=obuf[:, 2 * m:2 * m + 2, :].rearrange("p h d -> p (h d)"), identity=ident[:]),
                            reads=[('obuf', m // 2), 'ident'], writes=[('ps', bank)])
                    o_ = oTs[i % 2]
                    ok = 'oTs%d' % (i % 2)
                    sch.op('act', lambda pT=pT, o_=o_: nc.scalar.copy(out=o_[:].rearrange("p m q -> p (m q)"), in_=pT),
                           reads=[('ps', bank)], writes=[ok])
                    sch.dma('pool', oT[:, i * 128:(i + 1) * 128].rearrange("(m p) t -> p m t", p=128), o_[:], reads=[ok])
                sch.barrier()

        stg = stages if stages is not None else ALL_STAGES
        if 'mod' in stg:
            phase_mod()
        if 'l0x1' in stg:
            phase_x1(0, ab_w_in, 2560, xT, ctxT, px, pc, 0)
        if 'lru' in stg:
            phase_lru()
        if 'hyx' in stg:
            hyena_all(S, 'hx_', px, ymix)
        if 'hyc' in stg:
            hyena_all(C, 'hc_', pc, ymixc)
        if 'l0x2' in stg:
            phase_x2(0, ab_w_out, ymix, ymixc, xT, ctxT, xb0, ctxb0, True)
        if 'l1x1' in stg:
            def wload_qkv(w):
                for k in range(KC):
                    for P_ in range(2):
                        for hh in range(2):
                            sch.dma('pool', w[:, k, P_ * 512:(P_ + 1) * 512].rearrange("p (j h d) -> p j h d", j=4, h=2)[:, :, hh, :],
                                    at_w_qkv[k * 128:(k + 1) * 128, P_ * 512 + hh * 256:P_ * 512 + (hh + 1) * 256].rearrange("p (j d) -> p j d", j=4),
                                    writes=[('x1_w', k)])
                    sch.dma('pool', w[:, k, 1024:1280], at_w_qkv[k * 128:(k + 1) * 128, 1024:1280], writes=[('x1_w', k)])
            phase_x1(1, None, 1280, xb0, ctxb0, qkraw, qkraw_c, 0, gs=5, wload=wload_qkv, vproj=(at_w_qkv, vtok, vctok))
        if 'l1rope' in stg:
            phase_qkrope(qkraw, qkraw_c, qr, kr, kcr)
        if 'l1att' in stg:
            phase_att(qr, kr, kcr, vtok, vctok, oT)
        if 'l1x2' in stg:
            phase_x2(1, at_w_o, oT, None, xb0, None, outT, None, False)
        sch.barrier()
        kb.ninst = sch.ninst
    return kb


def make_in_map(inp, b, names, consts):
    m = {}
    for nme in names:
        if nme == 'xT':
            m[nme] = np.ascontiguousarray(np.asarray(inp['x'][b], np.float32).T)
        elif nme == 'ctxT':
            m[nme] = np.ascontiguousarray(np.asarray(inp['ctx'][b], np.float32).T)
        elif nme == 'smallp':
            m[nme] = build_small(inp, b)
        elif nme in consts:
            m[nme] = consts[nme]
        elif nme in ('ab_w_in', 'ab_w_out', 'at_w_qkv', 'at_w_o', 'hy_f_w1', 'hy_f_w2', 'hy_f_w3', 'lru_w_a', 'lru_w_i'):
            m[nme] = np.ascontiguousarray(np.asarray(inp[nme][0], np.float32))
        else:
            m[nme] = np.ascontiguousarray(np.asarray(inp[nme], np.float32))
    return m


ALL_STAGES = ['mod', 'l0x1', 'lru', 'hyx', 'hyc', 'l0x2', 'l1x1', 'l1rope', 'l1att', 'l1x2']


def kernel(**inputs):
    kb = build_program(dbg=(), stages=ALL_STAGES)
    consts = make_consts()
    names = list(kb.din.keys())
    in_maps = [make_in_map(inputs, b, names, consts) for b in range(8)]
    res = run_bass_kernel_spmd(kb.nc, in_maps, core_ids=list(range(8)))
    out = np.stack([np.ascontiguousarray(np.asarray(res.results[b]['outT'], np.float32).T) for b in range(8)], 0)
    return out
```

```python
import math
import numpy as np
import ml_dtypes
import concourse.bass as bass
import concourse.mybir as mybir
from concourse.bass_utils import run_bass_kernel_spmd
from contextlib import ExitStack

F32 = mybir.dt.float32
BF16 = mybir.dt.bfloat16
AF = mybir.ActivationFunctionType
ALU = mybir.AluOpType

D = 1024
S = 8192
C = 256
KC = 8
DFF = 2816
FC = 22
EPS = 1e-6
NB = 512
NK1 = 68
NG = 17


class Sch:
    CH = 30000
    POOL = {'sp': 40, 'act': 8, 'pool': 24}

    def __init__(self, nc, es):
        self.nc, self.es = nc, es
        self.E = {'pe': nc.tensor, 'dve': nc.vector, 'act': nc.scalar, 'pool': nc.gpsimd, 'sp': nc.sync}
        self.n = {e: 0 for e in self.E}
        self.csem = {e: [] for e in self.E}
        self.seen = {e: {} for e in self.E}
        self.lastw = {}
        self.rd = {}
        self.dpool = {q: [] for q in self.POOL}
        self.dnext = {q: 0 for q in self.POOL}
        self.semobj = []
        self.ninst = 0

    def _newsem(self, name):
        s = self.es.enter_context(self.nc.semaphore(name))
        self.semobj.append(s)
        return len(self.semobj) - 1

    def _wait(self, e, tok):
        sid, val = tok
        if self.seen[e].get(sid, 0) >= val:
            return
        self.E[e].wait_ge(self.semobj[sid], val)
        self.seen[e][sid] = val

    def _deps(self, e, reads, writes):
        toks = {}

        def add(t):
            if t is not None and toks.get(t[0], 0) < t[1]:
                toks[t[0]] = t[1]
        for r in reads:
            add(self.lastw.get(r))
        for w in writes:
            add(self.lastw.get(w))
            for sid, v in self.rd.get(w, {}).items():
                add((sid, v))
        own = set(self.csem[e]) if e == 'pe' else ()
        for sid, v in toks.items():
            if sid in own:
                continue
            self._wait(e, (sid, v))

    def _record(self, tok, reads, writes):
        for r in reads:
            d = self.rd.setdefault(r, {})
            if d.get(tok[0], 0) < tok[1]:
                d[tok[0]] = tok[1]
        for w in writes:
            self.lastw[w] = tok
            self.rd[w] = {}

    def op(self, e, fn, reads=(), writes=()):
        self._deps(e, reads, writes)
        ins = fn()
        k = self.n[e]
        ci = k // self.CH
        if ci >= len(self.csem[e]):
            self.csem[e].append(self._newsem("c_%s_%d" % (e, ci)))
        sid = self.csem[e][ci]
        ins.then_inc(self.semobj[sid], 1)
        self.n[e] += 1
        self.ninst += 1
        tok = (sid, k % self.CH + 1)
        self._record(tok, reads, writes)
        return tok

    def dma(self, q, out, in_, reads=(), writes=(), **kw):
        self._deps(q, reads, writes)
        pool = self.dpool[q]
        if len(pool) < self.POOL[q]:
            pool.append([self._newsem("d_%s_%d" % (q, len(pool))), 0])
            idx = len(pool) - 1
        else:
            idx = self.dnext[q] % self.POOL[q]
        self.dnext[q] += 1
        sid, v = pool[idx]
        if v > 0:
            self._wait(q, (sid, v))
        ins = self.E[q].dma_start(out=out, in_=in_, **kw)
        ins.then_inc(self.semobj[sid], 16)
        pool[idx][1] = v + 16
        self.ninst += 1
        tok = (sid, v + 16)
        self._record(tok, reads, writes)
        return tok

    def barrier(self):
        toks = []
        for e in self.E:
            if self.n[e] > 0:
                k = self.n[e] - 1
                toks.append((self.csem[e][k // self.CH], k % self.CH + 1))
        for q, pool in self.dpool.items():
            for sid, v in pool:
                if v > 0:
                    toks.append((sid, v))
        for e in self.E:
            for t in toks:
                self._wait(e, t)
        self.lastw.clear()
        self.rd.clear()


SMALL_ITEMS = [('c', 16), ('norm1', 16), ('norm2', 16), ('bmod', 96), ('hy_cw', 36), ('hy_cb', 12),
               ('hy_bias', 8), ('lru_cw', 16), ('lru_cb', 4), ('lru_ba', 8), ('lru_bi', 8), ('lru_lam', 8),
               ('qgain', 1), ('kgain', 1), ('sink', 16), ('hyf_b1', 1), ('hyf_b2', 1), ('hyf_freq', 1)]


def small_offsets():
    off = {}
    o = 0
    for k, n in SMALL_ITEMS:
        off[k] = (o, n)
        o += n
    return off, o


def pk(v, nch):
    return np.ascontiguousarray(np.asarray(v, np.float32).reshape(nch, 128).T)


def build_small(inp, b):
    off, tot = small_offsets()
    sm = np.zeros((128, tot), np.float32)

    def put(name, arr):
        o, n = off[name]
        assert arr.shape == (128, n), (name, arr.shape, n)
        sm[:, o:o + n] = arr
    cc = np.zeros((128, 16), np.float32)
    cc[:, 0::2] = pk(inp['c'][b], 8)
    cc[:, 1::2] = pk(inp['c_ctx'], 8)
    put('c', cc)
    put('norm1', np.concatenate([pk(inp['norm1'][i], 8) for i in range(2)], 1))
    put('norm2', np.concatenate([pk(inp['norm2'][i], 8) for i in range(2)], 1))
    put('bmod', np.concatenate([pk(inp['b_mod'][i], 48) for i in range(2)], 1))
    cw = inp['hy_conv_w'][0]
    a = np.zeros((128, 12, 3), np.float32)
    for j in range(3):
        a[:, :, j] = pk(cw[j], 12)
    put('hy_cw', a.reshape(128, 36))
    put('hy_cb', pk(inp['hy_conv_b'][0], 12))
    put('hy_bias', np.concatenate([pk(inp['hy_bias'][0, n], 4) for n in range(2)], 1))
    lw = inp['lru_conv_w'][0]
    a = np.zeros((128, 4, 4), np.float32)
    for j in range(4):
        a[:, :, j] = pk(lw[j], 4)
    put('lru_cw', a.reshape(128, 16))
    put('lru_cb', pk(inp['lru_conv_b'][0], 4))
    put('lru_ba', np.concatenate([pk(inp['lru_b_a'][0, d], 4) for d in range(2)], 1))
    put('lru_bi', np.concatenate([pk(inp['lru_b_i'][0, d], 4) for d in range(2)], 1))
    put('lru_lam', np.concatenate([pk(inp['lru_lam'][0, d], 4) for d in range(2)], 1))
    put('qgain', np.tile(np.asarray(inp['at_q_gain'][0], np.float32), 2)[:, None])
    put('kgain', np.tile(np.asarray(inp['at_k_gain'][0], np.float32), 2)[:, None])
    put('sink', np.tile(np.asarray(inp['at_sink'][0], np.float32)[None, :], (128, 1)))
    for nm, key in (('hyf_b1', 'hy_f_b1'), ('hyf_b2', 'hy_f_b2'), ('hyf_freq', 'hy_f_freq')):
        v = np.zeros((128, 1), np.float32)
        v[:64, 0] = inp[key][0]
        put(nm, v)
    return sm


def hy_params(L):
    N = 2 * L
    N2 = N // 128
    NH = N2 // 2
    K1 = N2 // 2 + 1
    NK1 = ((K1 + 3) // 4) * 4
    FB = min(512, L)
    return dict(L=L, N=N, N2=N2, NH=NH, K1=K1, NK1=NK1, NG=NK1 // 4, FB=FB, NBLK=N // FB)


def bf(a):
    return np.ascontiguousarray(np.asarray(a, np.float32).astype(ml_dtypes.bfloat16))


def hy_consts(L, pre):
    P = hy_params(L)
    N, N2, NH, NK1, FB, NBLK = P['N'], P['N2'], P['NH'], P['NK1'], P['FB'], P['NBLK']
    c = {}
    n2 = np.arange(N2)[:, None]
    k1 = np.arange(NK1)[None, :]
    th = 2 * np.pi * n2 * k1 / N2
    f1 = np.stack([np.sin(th), np.cos(th), -np.sin(th)], -1).reshape(N2, NK1 * 3)
    c[pre + 'f1tab'] = bf(f1)
    n1 = np.arange(128)[:, None, None]
    kk = np.arange(NK1)[None, :, None] + N2 * np.arange(128)[None, None, :]
    th = 2 * np.pi * ((n1 * kk) % N) / N
    c[pre + 'gtab'] = bf(np.stack([np.cos(th), -np.sin(th)], 1))
    tht = np.transpose(th, (2, 1, 0))
    c[pre + 'gttab'] = bf(np.stack([np.cos(tht), np.sin(tht)], 1))
    w = np.zeros(NK1)
    w[0] = 1.0
    w[N2 // 2] = 1.0
    w[1:N2 // 2] = 2.0
    ph = 2 * np.pi * np.arange(NK1)[:, None] * np.arange(NH)[None, :] / N2
    e = np.stack([np.cos(ph), -np.sin(ph)], 1) * (w[:, None, None] / N)
    c[pre + 'etab'] = bf(e)
    p = np.arange(N)
    lag = np.where(p < L, p, N - p).astype(np.float64)
    lag[L] = 0
    t = lag / (L - 1)
    bands = np.linspace(1e-4, 15.0, 16)
    wv = 2 * np.pi * lag / L
    z = np.concatenate([t[None, :], np.cos(bands[:, None] * wv[None, :]), -np.sin(bands[:, None] * wv[None, :])], 0)
    c[pre + 'zfeat'] = np.ascontiguousarray(z.astype(np.float32))
    min_decay = math.log(1e-2) / 1.5
    max_decay = math.log(1e-2) / 0.3
    deltas = np.abs(np.linspace(min_decay, max_decay, 512)) / (L - 1)
    c[pre + 'ndelta'] = pk(-deltas, 4)
    lagmin = np.array([(j * FB) if (j * FB) < L else (N - j * FB - FB + 1) for j in range(NBLK)], np.float32)
    c[pre + 'lagmin'] = np.ascontiguousarray(np.tile(lagmin[None, :], (128, 1)))
    c[pre + 'iota'] = np.ascontiguousarray(np.tile(np.arange(FB, dtype=np.float32)[None, :], (128, 1)))
    return c


def att_consts():
    c = {}
    p = np.arange(128)
    d = p % 64
    i = d % 16
    inv = 10000.0 ** (-(i.astype(np.float64)) / 16.0)
    t = np.arange(S)
    row = t // 64
    col = t % 64
    pos = np.where((d < 32)[:, None], row[None, :], col[None, :]).astype(np.float64)
    ang = pos * inv[:, None]
    c['rope_cos'] = np.ascontiguousarray(np.cos(ang).astype(np.float32))
    c['rope_sin'] = np.ascontiguousarray(np.sin(ang).astype(np.float32))
    R = np.zeros((128, 128), np.float32)
    for dst in range(128):
        if (dst % 32) < 16:
            R[dst + 16, dst] = -1.0
        else:
            R[dst - 16, dst] = 1.0
    c['rope_R'] = bf(R)
    bd = np.zeros((128, 128), np.float32)
    bd[:64, :64] = 1.0 / 64
    bd[64:, 64:] = 1.0 / 64
    c['bd64'] = bf(bd)
    k = np.arange(128)[:, None]
    q = np.arange(128)[None, :]
    m = np.stack([(k >= q), (k <= q)], 1).astype(np.float32)
    c['att_mask'] = bf(m)
    c['ident'] = bf(np.eye(128, dtype=np.float32))
    return c


_CONSTS = None


def make_consts():
    global _CONSTS
    if _CONSTS is None:
        c = {}
        c.update(hy_consts(S, 'hx_'))
        c.update(hy_consts(C, 'hc_'))
        c.update(att_consts())
        _CONSTS = c
    return _CONSTS


class KB:
    def __init__(self, dbg=()):
        self.dbg = set(dbg)
        self.nc = bass.Bass("TRN2", target_bir_lowering=False)
        self.es = ExitStack()
        self.sch = None
        self.din = {}
        self.dout = {}

    def inp(self, name, shape, dt=F32):
        t = self.nc.dram_tensor(name, list(shape), dt, kind="ExternalInput").ap()
        self.din[name] = t
        return t

    def scratch(self, name, shape, dt):
        kind = "ExternalOutput" if name in self.dbg else "Internal"
        t = self.nc.dram_tensor(name, list(shape), dt, kind=kind).ap()
        if name in self.dbg:
            self.dout[name] = t
        return t

    def sb(self, st, name, shape, dt):
        self.uid = getattr(self, 'uid', 0) + 1
        return st.enter_context(self.nc.sbuf_tensor("%s_u%d" % (name, self.uid), list(shape), dt))


def build_program(dbg=(), stages=None):
    kb = KB(dbg)
    nc = kb.nc
    off, nsm = small_offsets()
    xT = kb.inp("xT", [D, S])
    ctxT = kb.inp("ctxT", [D, C])
    smallp = kb.inp("smallp", [128, nsm])
    w_mod = kb.inp("w_mod", [2, D, 6 * D])
    ffn_w1 = kb.inp("ffn_w1", [2, D, DFF])
    ffn_w3 = kb.inp("ffn_w3", [2, D, DFF])
    ffn_w2 = kb.inp("ffn_w2", [2, DFF, D])
    ab_w_in = kb.inp("ab_w_in", [D, 2560])
    ab_w_out = kb.inp("ab_w_out", [D, D])
    lru_w_a = kb.inp("lru_w_a", [2, 8, 64, 64])
    lru_w_i = kb.inp("lru_w_i", [2, 8, 64, 64])
    hy_f_w1 = kb.inp("hy_f_w1", [33, 64])
    hy_f_w2 = kb.inp("hy_f_w2", [64, 64])
    hy_f_w3 = kb.inp("hy_f_w3", [64, 2048])
    at_w_qkv = kb.inp("at_w_qkv", [D, 1536])
    at_w_o = kb.inp("at_w_o", [D, D])
    ident_d = kb.inp("ident", [128, 128], BF16)
    outT = nc.dram_tensor("outT", [D, S], F32, kind="ExternalOutput").ap()
    kb.dout["outT"] = outT
    px = kb.scratch("px", [2560, S], BF16)
    pc = kb.scratch("pc", [2560, C], BF16)
    ymix = kb.scratch("ymix", [D, S], BF16)
    ymixc = kb.scratch("ymixc", [D, C], BF16)
    xb0 = kb.scratch("xb0", [D, S], F32)
    ctxb0 = kb.scratch("ctxb0", [D, C], F32)
    qkraw = kb.scratch("qkraw", [1280, S], BF16)
    qkraw_c = kb.scratch("qkraw_c", [1280, C], BF16)
    qr = kb.scratch("qr", [1024, S], BF16)
    kr = kb.scratch("kr", [256, S], BF16)
    kcr = kb.scratch("kcr", [256, C], BF16)
    vtok = kb.scratch("vtok", [S, 260], BF16)
    vctok = kb.scratch("vctok", [C, 260], BF16)
    oT = kb.scratch("oT", [D, S], BF16)

    with kb.es as es:
        sch = Sch(nc, es)
        kb.sch = sch
        sm = kb.sb(es, "sm", [128, nsm], F32)
        dsc = kb.sb(es, "dsc", [128, 2, 2, 6, 8], F32)
        ones_bf = kb.sb(es, "ones_bf", [128, 128], BF16)
        epsc = kb.sb(es, "epsc", [128, 1], F32)
        ps = [es.enter_context(nc.psum_tensor("ps%d" % i, [128, 512], F32)) for i in range(8)]
        st = {'psi': 0}

        def nextps():
            i = st['psi']
            st['psi'] = (i + 1) % 8
            return i

        sch.dma('sp', sm[:], smallp, writes=['sm'])
        sch.op('dve', lambda: nc.vector.memset(ones_bf[:], 1.0 / D), writes=['ones_bf'])
        sch.op('dve', lambda: nc.vector.memset(epsc[:], EPS), writes=['epsc'])

        def smc(name, j0=0, n=None):
            o, nn = off[name]
            if n is None:
                n = nn - j0
            return sm[:, o + j0:o + j0 + n]

        def phase_mod():
            with ExitStack() as ph:
                sc = kb.sb(ph, "p0_sc", [128, 16], F32)
                modv = kb.sb(ph, "p0_modv", [128, 2, 48, 2], F32)
                wp = [kb.sb(ph, "p0_wp%d" % i, [128, 8, 768], F32) for i in range(2)]
                sch.op('act', lambda: nc.scalar.activation(out=sc[:], in_=smc('c'), func=AF.Silu),
                       reads=['sm'], writes=['p0_sc'])
                cnt = 0
                for i in range(2):
                    pbank = nextps()
                    for pn in range(8):
                        w = wp[cnt % 2]
                        wk = 'p0_wp%d' % (cnt % 2)
                        cnt += 1
                        sch.dma('sp', w[:], w_mod[i][:, pn * 768:(pn + 1) * 768].rearrange("(k p) n -> p k n", p=128),
                                writes=[wk])
                        for ml in range(6):
                            m = pn * 6 + ml
                            for k in range(8):
                                sch.op('pe', lambda w=w, ml=ml, k=k, m=m, pbank=pbank: nc.tensor.matmul(
                                    out=ps[pbank][:, 2 * m:2 * m + 2], lhsT=w[:, k, ml * 128:(ml + 1) * 128],
                                    rhs=sc[:, 2 * k:2 * k + 2], start=(k == 0), stop=(k == 7)),
                                    reads=[wk, 'p0_sc'], writes=[('ps', pbank)])
                    o, _ = off['bmod']
                    sch.op('dve', lambda i=i, pbank=pbank, o=o: nc.vector.tensor_tensor(
                        out=modv[:, i, :, :], in0=ps[pbank][:, 0:96].rearrange("p (m s) -> p m s", s=2),
                        in1=sm[:, o + i * 48:o + (i + 1) * 48].unsqueeze(2).to_broadcast([128, 48, 2]), op=ALU.add),
                        reads=[('ps', pbank), 'sm'], writes=[('modv', i)])
                    for s in range(2):
                        n1 = smc('norm1', i * 8, 8)
                        n2 = smc('norm2', i * 8, 8)
                        rd = [('modv', i), 'sm']
                        sch.op('dve', lambda i=i, s=s, n1=n1: nc.vector.scalar_tensor_tensor(
                            out=dsc[:, i, s, 0, :], in0=modv[:, i, 8:16, s], scalar=1.0, in1=n1, op0=ALU.add, op1=ALU.mult),
                            reads=rd, writes=['dsc'])
                        sch.op('dve', lambda i=i, s=s, n2=n2: nc.vector.scalar_tensor_tensor(
                            out=dsc[:, i, s, 3, :], in0=modv[:, i, 32:40, s], scalar=1.0, in1=n2, op0=ALU.add, op1=ALU.mult),
                            reads=rd, writes=['dsc'])
                        for kind, c0 in ((1, 0), (2, 16), (4, 24), (5, 40)):
                            sch.op('dve', lambda i=i, s=s, kind=kind, c0=c0: nc.vector.tensor_copy(
                                out=dsc[:, i, s, kind, :], in_=modv[:, i, c0:c0 + 8, s]),
                                reads=rd, writes=['dsc'])
                sch.barrier()

        def load_w_bf16(dst, dst_key, src, nk, ncols):
            for k in range(nk):
                sch.dma('pool', dst[:, k, :], src[k * 128:(k + 1) * 128, :], writes=[(dst_key, k)],
                        max_dma_last_dim=2048)

        def norm_mod(xin, xkey, n, h, hkey, sq, sqkey, rstd, rkey, layer, stream, part, tmps, tkey):
            kA = 0 if part == 1 else 3
            for k in range(KC):
                sch.op('act', lambda k=k: nc.scalar.activation(out=sq[:, k, :n], in_=xin[:, k, :n], func=AF.Square),
                       reads=[(xkey, k)], writes=[(sqkey, k)])
            pb = nextps()
            for k in range(KC):
                sch.op('pe', lambda k=k, pb=pb: nc.tensor.matmul(out=ps[pb][:, :n], lhsT=ones_bf[:], rhs=sq[:, k, :n],
                                                                 start=(k == 0), stop=(k == KC - 1)),
                       reads=[(sqkey, k), 'ones_bf'], writes=[('ps', pb)])
            sch.op('act', lambda pb=pb: nc.scalar.activation(out=rstd[:, :n], in_=ps[pb][:, :n], func=AF.Ln,
                                                             bias=epsc[:, 0:1], scale=1.0),
                   reads=[('ps', pb), 'epsc'], writes=[rkey])
            sch.op('act', lambda: nc.scalar.activation(out=rstd[:, :n], in_=rstd[:, :n], func=AF.Exp, scale=-0.5),
                   reads=[rkey], writes=[rkey])
            for k in range(KC):
                tt = tmps[k % 2]
                tk = tkey + str(k % 2)
                sch.op('dve', lambda k=k, tt=tt: nc.vector.tensor_tensor(
                    out=tt[:, :n], in0=xin[:, k, :n], in1=rstd[:, :n], op=ALU.mult),
                    reads=[(xkey, k), rkey], writes=[tk])
                sch.op('act', lambda k=k, tt=tt: nc.scalar.activation(
                    out=h[:, k, :n], in_=tt[:, :n], func=AF.Identity,
                    bias=dsc[:, layer, stream, kA + 1, k:k + 1], scale=dsc[:, layer, stream, kA, k:k + 1]),
                    reads=[tk, 'dsc'], writes=[(hkey, k)])

        def phase_x1(layer, w_src, nout, xsrc, csrc, dst_x, dst_c, col0, gs=4, wload=None, vproj=None):
            nm = nout // 128
            with ExitStack() as ph:
                w = kb.sb(ph, "x1_w", [128, KC, nout], BF16)
                xin = kb.sb(ph, "x1_xin", [128, KC, NB], F32)
                sq = kb.sb(ph, "x1_sq", [128, KC, NB], BF16)
                h = kb.sb(ph, "x1_h", [128, KC, NB], BF16)
                rstd = kb.sb(ph, "x1_rstd", [128, NB], F32)
                tmps = [kb.sb(ph, "x1_tmp%d" % i, [128, NB], F32) for i in range(2)]
                ob = [kb.sb(ph, "x1_ob%d" % i, [128, gs, NB], BF16) for i in range(2)]
                if wload is None:
                    load_w_bf16(w, 'x1_w', w_src, KC, nout)
                else:
                    wload(w)
                if vproj is not None:
                    wv = kb.sb(ph, "x1_wv", [128, KC, 256], BF16)
                    vb = [kb.sb(ph, "x1_vb%d" % i, [128, 4, 65], BF16) for i in range(2)]
                    for k in range(KC):
                        sch.dma('pool', wv[:, k, :], vproj[0][k * 128:(k + 1) * 128, 1280:1536], writes=[('x1_wv', k)])
                    for i in range(2):
                        sch.op('dve', lambda i=i: nc.vector.memset(vb[i][:], 1.0), writes=['x1_vb%d' % i])
                    vcnt = 0
                blocks = [(1, csrc, dst_c, 0, C)] + [(0, xsrc, dst_x, j * NB, NB) for j in range(S // NB)]
                oc = 0
                for (stream, src, dst, t0, n) in blocks:
                    for k in range(KC):
                        sch.dma('sp', xin[:, k, :n], src[k * 128:(k + 1) * 128, t0:t0 + n], writes=[('x1_xin', k)])
                    norm_mod(xin, 'x1_xin', n, h, 'x1_h', sq, 'x1_sq', rstd, 'x1_rstd', layer, stream, 1, tmps, 'x1_tmp')
                    if vproj is not None:
                        vdst = vproj[2] if stream == 1 else vproj[1]
                        for sub in range(n // 128):
                            pb = nextps()
                            for k in range(KC):
                                sch.op('pe', lambda k=k, pb=pb, sub=sub: nc.tensor.matmul(
                                    out=ps[pb][:, 0:256], lhsT=h[:, k, sub * 128:(sub + 1) * 128], rhs=wv[:, k, :],
                                    start=(k == 0), stop=(k == KC - 1)),
                                    reads=[('x1_wv', k), ('x1_h', k)], writes=[('ps', pb)])
                            v_ = vb[vcnt % 2]
                            vk = 'x1_vb%d' % (vcnt % 2)
                            vcnt += 1
                            sch.op('dve', lambda v_=v_, pb=pb: nc.vector.tensor_copy(
                                out=v_[:, :, 0:64], in_=ps[pb][:, 0:256].rearrange("p (g d) -> p g d", d=64)),
                                reads=[('ps', pb)], writes=[vk])
                            sch.dma('pool', vdst[t0 + sub * 128:t0 + (sub + 1) * 128, :], v_[:].rearrange("p g d -> p (g d)"), reads=[vk])
                    for mg in range(nm // gs):
                        o = ob[oc % 2]
                        okey = 'x1_ob%d' % (oc % 2)
                        oc += 1
                        for ml in range(gs):
                            m = mg * gs + ml
                            pb = nextps()
                            for k in range(KC):
                                sch.op('pe', lambda k=k, m=m, pb=pb: nc.tensor.matmul(
                                    out=ps[pb][:, :n], lhsT=w[:, k, m * 128:(m + 1) * 128], rhs=h[:, k, :n],
                                    start=(k == 0), stop=(k == KC - 1)),
                                    reads=[('x1_w', k), ('x1_h', k)], writes=[('ps', pb)])
                            if ml % 2 == 0:
                                sch.op('dve', lambda o=o, ml=ml, pb=pb: nc.vector.tensor_copy(out=o[:, ml, :n], in_=ps[pb][:, :n]),
                                       reads=[('ps', pb)], writes=[(okey, ml)])
                            else:
                                sch.op('act', lambda o=o, ml=ml, pb=pb: nc.scalar.copy(out=o[:, ml, :n], in_=ps[pb][:, :n]),
                                       reads=[('ps', pb)], writes=[(okey, ml)])
                        sch.dma('pool', dst[mg * gs * 128:(mg + 1) * gs * 128, col0 + t0:col0 + t0 + n].rearrange("(m p) t -> p m t", p=128),
                                o[:, :, :n], reads=[(okey, ml) for ml in range(gs)])
                sch.barrier()

        def phase_x2(layer, wo_src, y_x, y_c, xsrc, csrc, dst_x, dst_c, do_ctx):
            with ExitStack() as ph:
                wo = kb.sb(ph, "x2_wo", [128, KC, D], BF16)
                w1 = kb.sb(ph, "x2_w1", [128, KC, DFF], BF16)
                w3 = kb.sb(ph, "x2_w3", [128, KC, DFF], BF16)
                w2 = kb.sb(ph, "x2_w2", [128, FC, D], BF16)
                xin = kb.sb(ph, "x2_xin", [128, KC, NB], F32)
                yb = kb.sb(ph, "x2_y", [128, KC, NB], BF16)
                u = kb.sb(ph, "x2_u", [128, FC, NB], BF16)
                sl = [kb.sb(ph, "x2_sl%d" % i, [128, NB], F32) for i in range(2)]
                rstd = kb.sb(ph, "x2_rstd", [128, NB], F32)
                sq = u
                load_w_bf16(wo, 'x2_wo', wo_src, KC, D)
                load_w_bf16(w1, 'x2_w1', ffn_w1[layer], KC, DFF)
                load_w_bf16(w3, 'x2_w3', ffn_w3[layer], KC, DFF)
                load_w_bf16(w2, 'x2_w2', ffn_w2[layer], FC, D)
                blocks = [(0, xsrc, y_x, dst_x, j * NB, NB) for j in range(S // NB)]
                if do_ctx:
                    blocks = [(1, csrc, y_c, dst_c, 0, C)] + blocks
                slc = 0
                for (stream, src, ysrc, dst, t0, n) in blocks:
                    for k in range(KC):
                        sch.dma('sp', xin[:, k, :n], src[k * 128:(k + 1) * 128, t0:t0 + n], writes=[('x2_xin', k)])
                    sch.dma('sp', yb[:, :, :n], ysrc[:, t0:t0 + n].rearrange("(k p) t -> p k t", p=128),
                            writes=[('x2_y', k) for k in range(KC)])
                    for m in range(KC):
                        pb = nextps()
                        for k in range(KC):
                            sch.op('pe', lambda k=k, m=m, pb=pb: nc.tensor.matmul(
                                out=ps[pb][:, :n], lhsT=wo[:, k, m * 128:(m + 1) * 128], rhs=yb[:, k, :n],
                                start=(k == 0), stop=(k == KC - 1)),
                                reads=[('x2_wo', k), ('x2_y', k)], writes=[('ps', pb)])
                        sch.op('dve', lambda m=m, pb=pb: nc.vector.scalar_tensor_tensor(
                            out=xin[:, m, :n], in0=ps[pb][:, :n], scalar=dsc[:, layer, stream, 2, m:m + 1],
                            in1=xin[:, m, :n], op0=ALU.mult, op1=ALU.add),
                            reads=[('ps', pb), ('x2_xin', m), 'dsc'], writes=[('x2_xin', m)])
                    h = yb
                    norm_mod(xin, 'x2_xin', n, h, 'x2_y', sq, 'x2_u', rstd, 'x2_rstd', layer, stream, 2, sl, 'x2_sl')
                    for f in range(FC):
                        pb1 = nextps()
                        for k in range(KC):
                            sch.op('pe', lambda k=k, f=f, pb1=pb1: nc.tensor.matmul(
                                out=ps[pb1][:, :n], lhsT=w1[:, k, f * 128:(f + 1) * 128], rhs=h[:, k, :n],
                                start=(k == 0), stop=(k == KC - 1)),
                                reads=[('x2_w1', k), ('x2_y', k)], writes=[('ps', pb1)])
                        pb3 = nextps()
                        for k in range(KC):
                            sch.op('pe', lambda k=k, f=f, pb3=pb3: nc.tensor.matmul(
                                out=ps[pb3][:, :n], lhsT=w3[:, k, f * 128:(f + 1) * 128], rhs=h[:, k, :n],
                                start=(k == 0), stop=(k == KC - 1)),
                                reads=[('x2_w3', k), ('x2_y', k)], writes=[('ps', pb3)])
                        s_ = sl[slc % 2]
                        skey = 'x2_sl%d' % (slc % 2)
                        slc += 1
                        sch.op('act', lambda s_=s_, pb1=pb1: nc.scalar.activation(out=s_[:, :n], in_=ps[pb1][:, :n], func=AF.Silu),
                               reads=[('ps', pb1)], writes=[skey])
                        sch.op('dve', lambda s_=s_, pb3=pb3, f=f: nc.vector.tensor_tensor(
                            out=u[:, f, :n], in0=ps[pb3][:, :n], in1=s_[:, :n], op=ALU.mult),
                            reads=[('ps', pb3), skey], writes=[('x2_u', f)])
                    for m in range(KC):
                        pb = nextps()
                        for f in range(FC):
                            sch.op('pe', lambda f=f, m=m, pb=pb: nc.tensor.matmul(
                                out=ps[pb][:, :n], lhsT=w2[:, f, m * 128:(m + 1) * 128], rhs=u[:, f, :n],
                                start=(f == 0), stop=(f == FC - 1)),
                                reads=[('x2_w2', f), ('x2_u', f)], writes=[('ps', pb)])
                        sch.op('dve', lambda m=m, pb=pb: nc.vector.scalar_tensor_tensor(
                            out=xin[:, m, :n], in0=ps[pb][:, :n], scalar=dsc[:, layer, stream, 5, m:m + 1],
                            in1=xin[:, m, :n], op0=ALU.mult, op1=ALU.add),
                            reads=[('ps', pb), ('x2_xin', m), 'dsc'], writes=[('x2_xin', m)])
                        sch.dma('pool', dst[m * 128:(m + 1) * 128, t0:t0 + n], xin[:, m, :n], reads=[('x2_xin', m)])
                sch.barrier()


        def gelu_tanh(src, n, t1, t1k, t2, t2k, srck):
            sch.op('act', lambda: nc.scalar.activation(out=t1[:, :n], in_=src, func=AF.Square), reads=srck, writes=t1k)
            sch.op('dve', lambda: nc.vector.tensor_scalar(out=t1[:, :n], in0=t1[:, :n], scalar1=0.044715, scalar2=1.0,
                                                          op0=ALU.mult, op1=ALU.add), reads=t1k, writes=t1k)
            sch.op('dve', lambda: nc.vector.tensor_tensor(out=t1[:, :n], in0=t1[:, :n], in1=src, op=ALU.mult),
                   reads=t1k + srck, writes=t1k)
            sch.op('act', lambda: nc.scalar.activation(out=t1[:, :n], in_=t1[:, :n], func=AF.Sigmoid, scale=1.5957691216057308),
                   reads=t1k, writes=t1k)
            sch.op('dve', lambda: nc.vector.tensor_tensor(out=t2[:, :n], in0=t1[:, :n], in1=src, op=ALU.mult),
                   reads=t1k + srck, writes=t2k)

        def phase_lru():
            T = 2048
            NSEG = S // T
            with ExitStack() as ph:
                cA = kb.sb(ph, "lr_cA", [128, 2, 8], F32)
                wblk = kb.sb(ph, "lr_w", [128, 2, 2, 128], BF16)
                identb = kb.sb(ph, "lr_id", [128, 128], BF16)
                dg = kb.sb(ph, "lr_dg", [128, 4, 128], BF16)
                rx = kb.sb(ph, "lr_rx", [128, S], F32)
                hcs = kb.sb(ph, "lr_hcs", [128, 2, C], F32)
                carry = kb.sb(ph, "lr_carry", [128, 2], F32)
                BS = []
                for i in range(3):
                    BS.append(dict(
                        i=i,
                        raw=kb.sb(ph, "lr_raw%d" % i, [128, T + 3], BF16),
                        lxc=kb.sb(ph, "lr_lxc%d" % i, [128, T], BF16),
                        A=kb.sb(ph, "lr_A%d" % i, [128, T], F32),
                        B=kb.sb(ph, "lr_B%d" % i, [128, T], F32),
                        tmp=kb.sb(ph, "lr_tmp%d" % i, [128, T], F32),
                        gx=kb.sb(ph, "lr_gx%d" % i, [128, T], BF16),
                        ob=kb.sb(ph, "lr_ob%d" % i, [128, T], BF16)))
                sch.dma('sp', identb[:], ident_d, writes=['lr_id'])
                lam = smc('lru_lam')
                sch.op('act', lambda: nc.scalar.activation(out=cA[:, 0, :], in_=lam, func=AF.Exp, scale=-1.0),
                       reads=['sm'], writes=['lr_cA'])
                sch.op('act', lambda: nc.scalar.activation(out=cA[:, 0, :], in_=cA[:, 0, :], func=AF.Ln, bias=1.0, scale=1.0),
                       reads=['lr_cA'], writes=['lr_cA'])
                sch.op('dve', lambda: nc.vector.tensor_scalar(out=cA[:, 1, :], in0=cA[:, 0, :], scalar1=-16.0, scalar2=None,
                                                              op0=ALU.mult), reads=['lr_cA'], writes=['lr_cA'])
                sch.op('dve', lambda: nc.vector.tensor_scalar(out=cA[:, 0, :], in0=cA[:, 0, :], scalar1=-8.0, scalar2=None,
                                                              op0=ALU.mult), reads=['lr_cA'], writes=['lr_cA'])
                ocw, _ = off['lru_cw']
                ocb, _ = off['lru_cb']
                oa, _ = off['lru_ba']
                oi, _ = off['lru_bi']
                cnt = [0]

                def K(bs, nm):
                    return 'lr_%s%d' % (nm, bs['i'])

                def conv4(bs, q, n):
                    raw, lxc = bs['raw'], bs['lxc']
                    for b0 in range(0, n, NB):
                        nn = min(NB, n - b0)
                        pb = nextps()
                        for j in range(4):
                            sch.op('pe', lambda pb=pb, j=j, b0=b0, nn=nn: nc.tensor.matmul(
                                out=ps[pb][:, :nn], lhsT=dg[:, j, :], rhs=raw[:, j + b0:j + b0 + nn], start=(j == 0), stop=(j == 3)),
                                reads=[K(bs, 'raw'), 'lr_dg'], writes=[('ps', pb)])
                        sch.op('act', lambda pb=pb, b0=b0, nn=nn: nc.scalar.activation(
                            out=lxc[:, b0:b0 + nn], in_=ps[pb][:, :nn], func=AF.Identity, bias=sm[:, ocb + q:ocb + q + 1], scale=1.0),
                            reads=[('ps', pb), 'sm'], writes=[(K(bs, 'lxc'), b0)])
                    return [(K(bs, 'lxc'), b0) for b0 in range(0, n, NB)]

                def gates(bs, q, d, n):
                    lxc, A, Bt = bs['lxc'], bs['A'], bs['B']
                    for b0 in range(0, n, NB):
                        nn = min(NB, n - b0)
                        pa = nextps()
                        sch.op('pe', lambda pa=pa, b0=b0, nn=nn: nc.tensor.matmul(
                            out=ps[pa][:, :nn], lhsT=wblk[:, d, 0, :], rhs=lxc[:, b0:b0 + nn], start=True, stop=True),
                            reads=['lr_w', (K(bs, 'lxc'), b0)], writes=[('ps', pa)])
                        pi = nextps()
                        sch.op('pe', lambda pi=pi, b0=b0, nn=nn: nc.tensor.matmul(
                            out=ps[pi][:, :nn], lhsT=wblk[:, d, 1, :], rhs=lxc[:, b0:b0 + nn], start=True, stop=True),
                            reads=['lr_w', (K(bs, 'lxc'), b0)], writes=[('ps', pi)])
                        sch.op('act', lambda pa=pa, b0=b0, nn=nn: nc.scalar.activation(
                            out=A[:, b0:b0 + nn], in_=ps[pa][:, :nn], func=AF.Sigmoid,
                            bias=sm[:, oa + d * 4 + q:oa + d * 4 + q + 1], scale=1.0),
                            reads=[('ps', pa), 'sm'], writes=[(K(bs, 'A'), b0)])
                        sch.op('act', lambda pi=pi, b0=b0, nn=nn: nc.scalar.activation(
                            out=Bt[:, b0:b0 + nn], in_=ps[pi][:, :nn], func=AF.Sigmoid,
                            bias=sm[:, oi + d * 4 + q:oi + d * 4 + q + 1], scale=1.0),
                            reads=[('ps', pi), 'sm'], writes=[(K(bs, 'B'), b0)])

                def coeffs2(bs, q, d, n, lk):
                    lxc, A, Bt, tmp = bs['lxc'], bs['A'], bs['B'], bs['tmp']
                    allA = [(K(bs, 'A'), b0) for b0 in range(0, n, NB)]
                    allB = [(K(bs, 'B'), b0) for b0 in range(0, n, NB)]
                    tk = [K(bs, 'tmp')]
                    sch.op('dve', lambda: nc.vector.tensor_tensor(out=Bt[:, :n], in0=Bt[:, :n], in1=lxc[:, :n], op=ALU.mult),
                           reads=allB + lk, writes=allB)
                    sch.op('act', lambda: nc.scalar.activation(out=tmp[:, :n], in_=A[:, :n], func=AF.Exp,
                                                               scale=cA[:, 1, d * 4 + q:d * 4 + q + 1]),
                           reads=allA + ['lr_cA'], writes=tk)
                    sch.op('act', lambda: nc.scalar.activation(out=A[:, :n], in_=A[:, :n], func=AF.Exp,
                                                               scale=cA[:, 0, d * 4 + q:d * 4 + q + 1]),
                           reads=allA + ['lr_cA'], writes=allA)
                    sch.op('act', lambda: nc.scalar.activation(out=tmp[:, :n], in_=tmp[:, :n], func=AF.Sqrt, scale=-1.0, bias=1.0),
                           reads=tk, writes=tk)
                    sch.op('dve', lambda: nc.vector.tensor_tensor(out=Bt[:, :n], in0=Bt[:, :n], in1=tmp[:, :n], op=ALU.mult),
                           reads=allB + tk, writes=allB)
                    return allA, allB

                def scan(bs, d, n, out_ap, outk, init, initk, allA, allB):
                    A, Bt = bs['A'], bs['B']
                    if d == 0:
                        f = lambda: nc.vector.tensor_tensor_scan(out=out_ap, data0=A[:, :n], data1=Bt[:, :n], initial=init,
                                                                 op0=ALU.mult, op1=ALU.add)
                    else:
                        f = lambda: nc.vector.tensor_tensor_scan(out=out_ap[:, ::-1], data0=A[:, :n][:, ::-1],
                                                                 data1=Bt[:, :n][:, ::-1], initial=init, op0=ALU.mult, op1=ALU.add)
                    sch.op('dve', f, reads=allA + allB + initk, writes=outk)

                def nextbs():
                    b = BS[cnt[0] % 3]
                    cnt[0] += 1
                    return b

                def pipeline(units):
                    for k in range(len(units)):
                        if k == 0:
                            units[0][0]()
                        if k + 1 < len(units):
                            units[k + 1][0]()
                        units[k][1]()

                for q in range(4):
                    sch.op('pool', lambda: nc.gpsimd.memset(wblk[:], 0.0), writes=['lr_w'])
                    for d in range(2):
                        for g, wsrc in ((0, lru_w_a), (1, lru_w_i)):
                            for hh in range(2):
                                sch.dma('pool', wblk[hh * 64:(hh + 1) * 64, d, g, hh * 64:(hh + 1) * 64], wsrc[d, 2 * q + hh],
                                        reads=[], writes=['lr_w'])
                    for j in range(4):
                        sch.op('dve', lambda j=j: nc.vector.tensor_scalar(
                            out=dg[:, j, :], in0=identb[:], scalar1=sm[:, ocw + q * 4 + j:ocw + q * 4 + j + 1], scalar2=None, op0=ALU.mult),
                            reads=['lr_id', 'sm'], writes=['lr_dg'])
                    row_l = 1536 + q * 128
                    row_g = 2048 + q * 128
                    units = []
                    state = {}

                    def mk_unit(d, kind, sg, first_of_dir, last_of_dir):
                        bs = nextbs()
                        n = C if kind == 'c' else T
                        st = {}

                        def stageA():
                            raw = bs['raw']
                            rk = K(bs, 'raw')
                            if kind == 'c':
                                sch.op('pool', lambda: nc.gpsimd.memset(raw[:, 0:2], 0.0), writes=[rk])
                                sch.op('pool', lambda: nc.gpsimd.memset(raw[:, 2 + C:3 + C], 0.0), writes=[rk])
                                sch.dma('sp', raw[:, 2:2 + C], pc[row_l:row_l + 128, :], writes=[rk])
                            else:
                                t0 = sg * T
                                lo = max(t0 - 2, 0)
                                hi = min(t0 + T + 1, S)
                                if lo > t0 - 2:
                                    sch.op('pool', lambda: nc.gpsimd.memset(raw[:, 0:2], 0.0), writes=[rk])
                                if hi < t0 + T + 1:
                                    sch.op('pool', lambda: nc.gpsimd.memset(raw[:, T + 2:T + 3], 0.0), writes=[rk])
                                d0 = lo - (t0 - 2)
                                sch.dma('sp', raw[:, d0:d0 + hi - lo], px[row_l:row_l + 128, lo:hi], writes=[rk])
                            lk = conv4(bs, q, n)
                            st['lk'] = lk
                            st['gk'] = gates(bs, q, d, n)

                        def stageB():
                            allA, allB = coeffs2(bs, q, d, n, st['lk'])
                            if kind == 'c':
                                scan(bs, d, C, hcs[:, d, :], [('lr_hcs', d)], 0.0, [], allA, allB)
                                state['init'] = hcs[:, d, C - 1:C] if d == 0 else hcs[:, d, 0:1]
                                state['initk'] = [('lr_hcs', d)]
                                return
                            t0 = sg * T
                            init, initk = state['init'], state['initk']
                            if d == 0:
                                scan(bs, d, T, rx[:, t0:t0 + T], [('lr_rx', sg)], init, initk, allA, allB)
                                if not last_of_dir:
                                    sch.op('dve', lambda: nc.vector.tensor_copy(out=carry[:, 0:1], in_=rx[:, t0 + T - 1:t0 + T]),
                                           reads=[('lr_rx', sg)] + initk, writes=[('lr_carry', 0)])
                                    state['init'], state['initk'] = carry[:, 0:1], [('lr_carry', 0)]
                            else:
                                tmp = bs['tmp']
                                tk = [K(bs, 'tmp')]
                                scan(bs, d, T, tmp[:, :T], tk, init, initk, allA, allB)
                                if not last_of_dir:
                                    sch.op('dve', lambda: nc.vector.tensor_copy(out=carry[:, 1:2], in_=tmp[:, 0:1]),
                                           reads=tk + initk, writes=[('lr_carry', 1)])
                                    state['init'], state['initk'] = carry[:, 1:2], [('lr_carry', 1)]
                                sch.op('dve', lambda: nc.vector.tensor_tensor(out=rx[:, t0:t0 + T], in0=rx[:, t0:t0 + T],
                                                                              in1=tmp[:, :T], op=ALU.add),
                                       reads=tk + [('lr_rx', sg)], writes=[('lr_rx', sg)])
                        return (stageA, stageB)

                    for d in range(2):
                        units.append(mk_unit(d, 'c', None, True, False))
                        segs = list(range(NSEG)) if d == 0 else list(range(NSEG - 1, -1, -1))
                        for si, sg in enumerate(segs):
                            units.append(mk_unit(d, 'x', sg, False, si == NSEG - 1))

                    def mk_gate(kind, sg):
                        bs = nextbs()
                        n = C if kind == 'c' else T
                        gxb, A, Bt, tmp, ob = bs['gx'], bs['A'], bs['B'], bs['tmp'], bs['ob']
                        ak = [(K(bs, 'A'), b0) for b0 in range(0, n, NB)]
                        tk = [K(bs, 'tmp')]
                        gk = [K(bs, 'gx')]

                        def stageA():
                            if kind == 'c':
                                sch.dma('sp', gxb[:, :C], pc[row_g:row_g + 128, :], writes=gk)
                            else:
                                sch.dma('sp', gxb[:, :T], px[row_g:row_g + 128, sg * T:(sg + 1) * T], writes=gk)
                            src_ = gxb[:, :n]
                            sch.op('act', lambda: nc.scalar.activation(out=A[:, :n], in_=src_, func=AF.Square), reads=gk, writes=ak)
                            sch.op('dve', lambda: nc.vector.tensor_scalar(out=A[:, :n], in0=A[:, :n], scalar1=0.044715, scalar2=1.0,
                                                                          op0=ALU.mult, op1=ALU.add), reads=ak, writes=ak)
                            sch.op('dve', lambda: nc.vector.tensor_tensor(out=A[:, :n], in0=A[:, :n], in1=src_, op=ALU.mult),
                                   reads=ak + gk, writes=ak)

                        def stageB():
                            src_ = gxb[:, :n]
                            sch.op('act', lambda: nc.scalar.activation(out=A[:, :n], in_=A[:, :n], func=AF.Sigmoid, scale=1.5957691216057308),
                                   reads=ak, writes=ak)
                            sch.op('dve', lambda: nc.vector.tensor_tensor(out=tmp[:, :n], in0=A[:, :n], in1=src_, op=ALU.mult),
                                   reads=ak + gk, writes=tk)
                            if kind == 'c':
                                sch.op('dve', lambda: nc.vector.tensor_tensor(out=Bt[:, :C], in0=hcs[:, 0, :], in1=hcs[:, 1, :], op=ALU.add),
                                       reads=[('lr_hcs', 0), ('lr_hcs', 1)], writes=[(K(bs, 'B'), 0)])
                                sch.op('dve', lambda: nc.vector.tensor_tensor(out=ob[:, :C], in0=Bt[:, :C], in1=tmp[:, :C], op=ALU.mult),
                                       reads=tk + [(K(bs, 'B'), 0)], writes=[K(bs, 'ob')])
                                sch.dma('pool', ymixc[512 + q * 128:512 + (q + 1) * 128, :], ob[:, :C], reads=[K(bs, 'ob')])
                            else:
                                t0 = sg * T
                                sch.op('dve', lambda: nc.vector.tensor_tensor(out=ob[:, :T], in0=rx[:, t0:t0 + T], in1=tmp[:, :T], op=ALU.mult),
                                       reads=tk + [('lr_rx', sg)], writes=[K(bs, 'ob')])
                                sch.dma('pool', ymix[512 + q * 128:512 + (q + 1) * 128, t0:t0 + T], ob[:, :T], reads=[K(bs, 'ob')])
                        return (stageA, stageB)

                    for sg in range(NSEG):
                        units.append(mk_gate('x', sg))
                    units.append(mk_gate('c', None))
                    pipeline(units)
                sch.barrier()

        def hyena_all(L, pre, src, dst):
            P = hy_params(L)
            N, N2, NH, NK1, NGr, FB, NBLK = P['N'], P['N2'], P['NH'], P['NK1'], P['NG'], P['FB'], P['NBLK']
            W3 = NK1 * 3
            f1tab_d = kb.inp(pre + "f1tab", [N2, W3], BF16)
            gtab_d = kb.inp(pre + "gtab", [128, 2, NK1, 128], BF16)
            gttab_d = kb.inp(pre + "gttab", [128, 2, NK1, 128], BF16)
            etab_d = kb.inp(pre + "etab", [NK1, 2, NH], BF16)
            zfeat_d = kb.inp(pre + "zfeat", [33, N])
            ndelta_d = kb.inp(pre + "ndelta", [128, 4])
            lagmin_d = kb.inp(pre + "lagmin", [128, NBLK])
            iota_d = kb.inp(pre + "iota", [128, FB])
            kt = kb.scratch(pre + "kt", [2, 512, N], BF16)
            kf = kb.scratch(pre + "kf", [2, 4, NGr, 128, 4 * 2 * 128], BF16)
            uc = kb.scratch(pre + "uc", [1536, L], BF16)
            z1 = kb.scratch(pre + "z1", [512, L], BF16)
            dd = kb.scratch(pre + "dd", [NK1, 2 * 128 * 128], BF16)
            TWO_PI = 2.0 * math.pi

            with ExitStack() as hp:
                hnrm = kb.sb(hp, pre + "hnrm", [128, 8], F32)
                hsc = kb.sb(hp, pre + "hsc", [128, 2, 8], F32)
                def fft_fwd(ph, rows_ap, nrow_k, epilogue):
                    Xs = kb.sb(ph, "Xs", [nrow_k, 128, 128], BF16)
                    Bp = kb.sb(ph, "Bp", [128, 128, W3], BF16)
                    v = rows_ap.rearrange("c (a b) -> a c b", b=128)
                    for c0 in range(0, 128, 32):
                        sch.dma('sp', Xs[:, c0:c0 + 32, :], v[:, c0:c0 + 32, :], writes=[('Xs', c0)])
                    cpb = 2 if W3 > 128 else 32
                    slot = 512 // cpb
                    ngrp = 128 // cpb
                    bpk = [('Bp', cp) for cp in range(ngrp)]
                    for cp in range(ngrp):
                        pb = nextps()
                        for cc in range(cpb):
                            c = cp * cpb + cc
                            sch.op('pe', lambda c=c, cc=cc, pb=pb: nc.tensor.matmul(
                                out=ps[pb][:, cc * slot:cc * slot + W3], lhsT=Xs[:, c, :], rhs=f1tab[0:nrow_k, :],
                                start=True, stop=True),
                                reads=[('Xs', (c // 32) * 32), 'f1tab'], writes=[('ps', pb)])
                        src_ap = ps[pb][:, :].rearrange("p (c w) -> p c w", c=cpb)[:, :, 0:W3]
                        dst_ap = Bp[:, cp * cpb:(cp + 1) * cpb, :]
                        if cp % 2 == 0:
                            sch.op('dve', lambda s_=src_ap, d_=dst_ap: nc.vector.tensor_copy(out=d_, in_=s_),
                                   reads=[('ps', pb)], writes=[('Bp', cp)])
                        else:
                            sch.op('act', lambda s_=src_ap, d_=dst_ap: nc.scalar.copy(out=d_, in_=s_),
                                   reads=[('ps', pb)], writes=[('Bp', cp)])
                    pending = [None]
                    for g in range(NGr):
                        ba = nextps()
                        bb = nextps()
                        for kl in range(4):
                            k1 = g * 4 + kl
                            pb = ba if kl < 2 else bb
                            cs = slice((kl % 2) * 256, (kl % 2) * 256 + 256)
                            for (lt, c0, st_) in ((0, 1, True), (1, 0, False)):
                                sch.op('pe', lambda pb=pb, lt=lt, c0=c0, st_=st_, k1=k1, cs=cs: nc.tensor.matmul(
                                    out=ps[pb][:, cs], lhsT=gtab[:, lt, k1, :],
                                    rhs=Bp[:, :, k1 * 3 + c0:k1 * 3 + c0 + 2].rearrange("p c m -> p m c"),
                                    start=st_, stop=not st_),
                                    reads=bpk + ['gtab'], writes=[('ps', pb)])
                        nb_ = epilogue(g, ba, bb)
                        if pending[0] is not None:
                            pending[0]()
                        pending[0] = nb_
                    if pending[0] is not None:
                        pending[0]()

                with ExitStack() as ph:
                    h2all = kb.sb(ph, "h2all", [64, N], F32)
                    w1s = kb.sb(ph, "w1s", [33, 64], F32)
                    w2s = kb.sb(ph, "w2s", [64, 64], F32)
                    w3s = kb.sb(ph, "w3s", [64, 2048], F32)
                    bfq = kb.sb(ph, "bfq", [64, 2], F32)
                    sch.dma('sp', w1s[:], hy_f_w1, writes=['w1s'])
                    sch.dma('sp', w2s[:], hy_f_w2, writes=['w2s'])
                    sch.dma('sp', w3s[:], hy_f_w3, writes=['w3s'])
                    fq = smc('hyf_freq')[0:64]
                    sch.op('dve', lambda: nc.vector.tensor_tensor(out=bfq[:, 0:1], in0=smc('hyf_b1')[0:64], in1=fq, op=ALU.mult),
                           reads=['sm'], writes=['bfq'])
                    sch.op('dve', lambda: nc.vector.tensor_tensor(out=bfq[:, 1:2], in0=smc('hyf_b2')[0:64], in1=fq, op=ALU.mult),
                           reads=['sm'], writes=['bfq'])

                    def sin_layer(pb, n, bcol, arg, argk, tt, ttk, out_ap, outk):
                        sch.op('act', lambda: nc.scalar.activation(out=arg[:, :n], in_=ps[pb][0:64, :n], func=AF.Identity,
                                                                   scale=fq, bias=bfq[:, bcol:bcol + 1]),
                               reads=[('ps', pb), 'bfq', 'sm'], writes=[argk])
                        for (cmp_, thr, sgn) in ((ALU.is_gt, math.pi, ALU.subtract), (ALU.is_lt, -math.pi, ALU.add)):
                            sch.op('dve', lambda cmp_=cmp_, thr=thr: nc.vector.tensor_scalar(
                                out=tt[:, :n], in0=arg[:, :n], scalar1=thr, scalar2=TWO_PI, op0=cmp_, op1=ALU.mult),
                                reads=[argk], writes=[ttk])
                            sch.op('dve', lambda sgn=sgn: nc.vector.tensor_tensor(out=arg[:, :n], in0=arg[:, :n], in1=tt[:, :n], op=sgn),
                                   reads=[argk, ttk], writes=[argk])
                        sch.op('act', lambda: nc.scalar.activation(out=out_ap, in_=arg[:, :n], func=AF.Sin),
                               reads=[argk], writes=[outk])

                    with ExitStack() as p1:
                        zb = [kb.sb(p1, "zb%d" % i, [33, FB], F32) for i in range(2)]
                        arg = kb.sb(p1, "arg", [64, FB], F32)
                        tt = kb.sb(p1, "tt", [64, FB], F32)
                        h1 = kb.sb(p1, "h1", [64, FB], F32)
                        for j in range(NBLK):
                            z_ = zb[j % 2]
                            zk = 'zb%d' % (j % 2)
                            sch.dma('sp', z_[:], zfeat_d[:, j * FB:(j + 1) * FB], writes=[zk])
                            pb = nextps()
                            sch.op('pe', lambda z_=z_, pb=pb: nc.tensor.matmul(out=ps[pb][0:64, :FB], lhsT=w1s[:], rhs=z_[:],
                                                                              start=True, stop=True),
                                   reads=[zk, 'w1s'], writes=[('ps', pb)])
                            sin_layer(pb, FB, 0, arg, 'arg', tt, 'tt', h1[:, :FB], 'h1')
                            pb = nextps()
                            sch.op('pe', lambda pb=pb: nc.tensor.matmul(out=ps[pb][0:64, :FB], lhsT=w2s[:], rhs=h1[:, :FB],
                                                                        start=True, stop=True),
                                   reads=['h1', 'w2s'], writes=[('ps', pb)])
                            sin_layer(pb, FB, 1, arg, 'arg', tt, 'tt', h2all[:, j * FB:(j + 1) * FB], ('h2', j))
                        sch.barrier()
                    with ExitStack() as p2:
                        ktb2 = kb.sb(p2, "ktb2", [128, 2, N], BF16)
                        wb = kb.sb(p2, "wb", [128, FB], F32)
                        scj = kb.sb(p2, "scj", [128, NBLK], F32)
                        wt = [kb.sb(p2, "wt%d" % i, [128, FB], F32) for i in range(2)]
                        junk = [kb.sb(p2, "junk%d" % i, [128, FB], BF16) for i in range(2)]
                        acc = kb.sb(p2, "acc", [128, 2, NBLK], F32)
                        ndl = kb.sb(p2, "ndl", [128, 4], F32)
                        lgm = kb.sb(p2, "lgm", [128, NBLK], F32)
                        iot = kb.sb(p2, "iot", [128, FB], F32)
                        sch.dma('sp', ndl[:], ndelta_d, writes=['ndl'])
                        sch.dma('sp', lgm[:], lagmin_d, writes=['lgm'])
                        sch.dma('sp', iot[:], iota_d, writes=['iot'])
                        wc = 0
                        jc = 0
                        for q in range(4):
                            sch.op('act', lambda q=q: nc.scalar.activation(out=wb[:], in_=iot[:], func=AF.Exp, scale=ndl[:, q:q + 1]),
                                   reads=['iot', 'ndl'], writes=['wb'])
                            sch.op('act', lambda q=q: nc.scalar.activation(out=scj[:], in_=lgm[:], func=AF.Exp, scale=ndl[:, q:q + 1]),
                                   reads=['lgm', 'ndl'], writes=['scj'])
                            sch.op('dve', lambda: nc.vector.memset(acc[:], 0.0), writes=['acc'])
                            for j in range(NBLK):
                                hf = 1 if j * FB >= L else 0
                                w_ = wt[wc % 2]
                                wk_ = 'wt%d' % (wc % 2)
                                wc += 1
                                wsrc = wb[:, ::-1] if hf else wb[:, :]
                                sch.op('dve', lambda w_=w_, wsrc=wsrc, j=j: nc.vector.tensor_scalar(
                                    out=w_[:], in0=wsrc, scalar1=scj[:, j:j + 1], scalar2=0.05, op0=ALU.mult, op1=ALU.add),
                                    reads=['wb', 'scj'], writes=[wk_])
                                for n in range(2):
                                    col = hf * 1024 + n * 512 + q * 128
                                    pb = nextps()
                                    sch.op('pe', lambda pb=pb, col=col, j=j: nc.tensor.matmul(
                                        out=ps[pb][:, :FB], lhsT=w3s[:, col:col + 128], rhs=h2all[:, j * FB:(j + 1) * FB],
                                        start=True, stop=True), reads=['w3s', ('h2', j)], writes=[('ps', pb)])
                                    kk_ = ('ktb2', n, j)
                                    sch.op('dve', lambda w_=w_, pb=pb, j=j, n=n: nc.vector.tensor_tensor(
                                        out=ktb2[:, n, j * FB:(j + 1) * FB], in0=ps[pb][:, :FB], in1=w_[:], op=ALU.mult),
                                        reads=[wk_, ('ps', pb)], writes=[kk_])
                                    if j * FB == L:
                                        sch.op('dve', lambda n=n: nc.vector.memset(ktb2[:, n, L:L + 1], 0.0), reads=[kk_], writes=[kk_])
                                    jk = junk[jc % 2]
                                    jkk = 'junk%d' % (jc % 2)
                                    jc += 1
                                    sch.op('act', lambda jk=jk, n=n, j=j: nc.scalar.activation(
                                        out=jk[:], in_=ktb2[:, n, j * FB:(j + 1) * FB], func=AF.Abs, accum_out=acc[:, n, j:j + 1]),
                                        reads=[kk_, 'acc'], writes=[jkk, ('acc', n, j)])
                            for n in range(2):
                                sch.op('dve', lambda n=n, q=q: nc.vector.tensor_reduce(
                                    out=hnrm[:, n * 4 + q:n * 4 + q + 1], in_=acc[:, n, :], axis=mybir.AxisListType.X, op=ALU.add),
                                    reads=[('acc', n, j) for j in range(NBLK)] + ['acc'], writes=[('hnrm', n, q)])
                                sch.dma('pool', kt[n, q * 128:(q + 1) * 128, :], ktb2[:, n, :], reads=[('ktb2', n, j) for j in range(NBLK)])
                        sch.barrier()
                sch.barrier()

                f1tab = kb.sb(hp, pre + "f1tab", [N2, W3], BF16)
                gtab = kb.sb(hp, pre + "gtab", [128, 2, NK1, 128], BF16)
                gttab = kb.sb(hp, pre + "gttab", [128, 2, NK1, 128], BF16)
                etab = kb.sb(hp, pre + "etab", [NK1, 2, NH], BF16)
                sch.dma('sp', f1tab[:], f1tab_d, writes=['f1tab'])
                for r in range(2):
                    sch.dma('sp', gtab[:, r], gtab_d[:, r], writes=['gtab'])
                    sch.dma('sp', gttab[:, r], gttab_d[:, r], writes=['gttab'])
                sch.dma('sp', etab[:], etab_d, writes=['etab'])

                for n in range(2):
                    for q in range(4):
                        with ExitStack() as ph:
                            ub = [kb.sb(ph, "ub%d" % i, [128, 4, 2, 128], BF16) for i in range(2)]

                            def ep_store(g, ba, bb, n=n, q=q, ub=ub):
                                u_ = ub[g % 2]
                                uk = 'ub%d' % (g % 2)
                                sch.op('dve', lambda: nc.vector.tensor_copy(
                                    out=u_[:, 0:2, :, :].rearrange("p k r c -> p (k r c)"), in_=ps[ba][:, :]),
                                    reads=[('ps', ba)], writes=[(uk, 0)])
                                sch.op('act', lambda: nc.scalar.copy(
                                    out=u_[:, 2:4, :, :].rearrange("p k r c -> p (k r c)"), in_=ps[bb][:, :]),
                                    reads=[('ps', bb)], writes=[(uk, 1)])
                                sch.dma('pool', kf[n, q, g], u_[:].rearrange("p k r c -> p (k r c)"), reads=[(uk, 0), (uk, 1)])
                            fft_fwd(ph, kt[n, q * 128:(q + 1) * 128, :], N2, ep_store)
                            sch.barrier()

                with ExitStack() as ph:
                    T = min(L, 2048)
                    raws = [kb.sb(ph, "hraw%d" % i, [128, T + 2], BF16) for i in range(2)]
                    obs = [kb.sb(ph, "hob%d" % i, [128, T], BF16) for i in range(2)]
                    idb = kb.sb(ph, "hidb", [128, 128], BF16)
                    dg3 = [kb.sb(ph, "hdg%d" % i, [128, 3, 128], BF16) for i in range(2)]
                    sch.dma('sp', idb[:], ident_d, writes=['hidb'])
                    ocw, _ = off['hy_cw']
                    ocb, _ = off['hy_cb']
                    it = 0
                    for qq in range(12):
                        dg = dg3[qq % 2]
                        dgk = 'hdg%d' % (qq % 2)
                        for j in range(3):
                            sch.op('dve', lambda j=j, dg=dg, qq=qq: nc.vector.tensor_scalar(
                                out=dg[:, j, :], in0=idb[:], scalar1=sm[:, ocw + qq * 3 + j:ocw + qq * 3 + j + 1], scalar2=None, op0=ALU.mult),
                                reads=['hidb', 'sm'], writes=[dgk])
                        for t0 in range(0, L, T):
                            raw = raws[it % 2]
                            rk = 'hraw%d' % (it % 2)
                            ob = obs[it % 2]
                            okk = 'hob%d' % (it % 2)
                            it += 1
                            lo = max(t0 - 1, 0)
                            hi = min(t0 + T + 1, L)
                            if t0 == 0:
                                sch.op('pool', lambda raw=raw: nc.gpsimd.memset(raw[:, 0:1], 0.0), writes=[rk])
                            if t0 + T >= L:
                                sch.op('pool', lambda raw=raw: nc.gpsimd.memset(raw[:, T + 1:T + 2], 0.0), writes=[rk])
                            d0 = lo - (t0 - 1)
                            sch.dma('sp', raw[:, d0:d0 + hi - lo], src[qq * 128:(qq + 1) * 128, lo:hi], writes=[rk])
                            for bi, b0 in enumerate(range(0, T, NB)):
                                nn = min(NB, T - b0)
                                pb = nextps()
                                for j in range(3):
                                    sch.op('pe', lambda pb=pb, j=j, b0=b0, nn=nn, raw=raw, dg=dg: nc.tensor.matmul(
                                        out=ps[pb][:, :nn], lhsT=dg[:, j, :], rhs=raw[:, j + b0:j + b0 + nn], start=(j == 0), stop=(j == 2)),
                                        reads=[rk, dgk], writes=[('ps', pb)])
                                if bi % 2 == 0:
                                    sch.op('act', lambda pb=pb, b0=b0, nn=nn, ob=ob, qq=qq: nc.scalar.activation(
                                        out=ob[:, b0:b0 + nn], in_=ps[pb][:, :nn], func=AF.Identity, bias=sm[:, ocb + qq:ocb + qq + 1], scale=1.0),
                                        reads=[('ps', pb), 'sm'], writes=[(okk, b0)])
                                else:
                                    sch.op('dve', lambda pb=pb, b0=b0, nn=nn, ob=ob, qq=qq: nc.vector.tensor_scalar(
                                        out=ob[:, b0:b0 + nn], in0=ps[pb][:, :nn], scalar1=sm[:, ocb + qq:ocb + qq + 1], scalar2=None, op0=ALU.add),
                                        reads=[('ps', pb), 'sm'], writes=[(okk, b0)])
                            sch.dma('pool', uc[qq * 128:(qq + 1) * 128, t0:t0 + T], ob[:, :T], reads=[(okk, b0) for b0 in range(0, T, NB)])
                    sch.barrier()

                obias, _ = off['hy_bias']
                hk = [('hnrm', n_, q_) for n_ in range(2) for q_ in range(4)]
                sch.op('dve', lambda: nc.vector.reciprocal(out=hsc[:, 0, :], in_=hnrm[:]), reads=hk, writes=['hsc'])
                sch.op('dve', lambda: nc.vector.tensor_tensor(out=hsc[:, 1, :], in0=hnrm[:], in1=sm[:, obias:obias + 8], op=ALU.mult),
                       reads=hk + ['sm'], writes=['hsc'])
                for n in range(2):
                    zin = uc[0:512, :] if n == 0 else z1
                    zout = z1 if n == 0 else dst
                    for q in range(4):
                        rows = zin[q * 128:(q + 1) * 128, :]
                        with ExitStack() as ph:
                            kfg = [kb.sb(ph, "kfg%d" % i, [128, 4, 2, 128], BF16) for i in range(2)]
                            Yt = [kb.sb(ph, "Yt%d" % i, [128, 4, 3, 128], BF16) for i in range(2)]
                            tq = [kb.sb(ph, "tq%d" % i, [128, 4, 2, 128], F32) for i in range(4)]
                            dst_ = [kb.sb(ph, "dst%d" % i, [128, 4, 2, 128], BF16) for i in range(2)]

                            def ep_conv(g, ba, bb, n=n, q=q, kfg=kfg, Yt=Yt, tq=tq, dst_=dst_):
                                kg = kfg[g % 2]
                                kk = 'kfg%d' % (g % 2)
                                Y = Yt[g % 2]
                                yk = 'Yt%d' % (g % 2)
                                P1 = tq[(g % 2) * 2]
                                P2 = tq[(g % 2) * 2 + 1]
                                p1k = ('tq', (g % 2) * 2)
                                p2k = ('tq', (g % 2) * 2 + 1)
                                sch.dma('sp', kg[:].rearrange("p k r c -> p (k r c)"), kf[n, q, g], writes=[kk])
                                for hb, pb in enumerate((ba, bb)):
                                    uv = ps[pb][:, :].rearrange("p (k r c) -> p k r c", k=2, r=2)
                                    ks = slice(hb * 2, hb * 2 + 2)
                                    sch.op('dve', lambda pb=pb, ks=ks: nc.vector.tensor_tensor(
                                        out=P1[:, ks].rearrange("p k r c -> p (k r c)"), in0=ps[pb][:, :],
                                        in1=kg[:, ks].rearrange("p k r c -> p (k r c)"), op=ALU.mult),
                                        reads=[('ps', pb), kk], writes=[(p1k, hb)])
                                    for r in range(2):
                                        sch.op('dve', lambda uv=uv, ks=ks, r=r: nc.vector.tensor_tensor(
                                            out=P2[:, ks, r, :], in0=uv[:, :, r, :], in1=kg[:, ks, 1 - r, :], op=ALU.mult),
                                            reads=[('ps', pb), kk], writes=[(p2k, hb, r)])
                                p1r = [(p1k, 0), (p1k, 1)]
                                p2r = [(p2k, hb, r) for hb in range(2) for r in range(2)]
                                sch.op('pool', lambda: nc.gpsimd.tensor_tensor(out=Y[:, :, 1, :], in0=P1[:, :, 0, :], in1=P1[:, :, 1, :], op=ALU.subtract),
                                       reads=p1r, writes=[(yk, 1)])
                                sch.op('pool', lambda: nc.gpsimd.tensor_tensor(out=Y[:, :, 2, :], in0=P2[:, :, 0, :], in1=P2[:, :, 1, :], op=ALU.add),
                                       reads=p2r, writes=[(yk, 2)])
                                sch.op('act', lambda: nc.scalar.mul(out=Y[:, :, 0, :], in_=Y[:, :, 2, :], mul=-1.0), reads=[(yk, 2)], writes=[(yk, 0)])

                                def part_b(g=g, Y=Y, yk=yk):
                                    da = nextps()
                                    db = nextps()
                                    for kl in range(4):
                                        k1 = g * 4 + kl
                                        pb = da if kl < 2 else db
                                        cs = slice((kl % 2) * 256, (kl % 2) * 256 + 256)
                                        for (lt, c0, st_) in ((0, 1, True), (1, 0, False)):
                                            sch.op('pe', lambda pb=pb, lt=lt, c0=c0, st_=st_, k1=k1, cs=cs, kl=kl: nc.tensor.matmul(
                                                out=ps[pb][:, cs], lhsT=gttab[:, lt, k1, :],
                                                rhs=Y[:, kl, c0:c0 + 2, :].rearrange("p m c -> p (m c)"), start=st_, stop=not st_),
                                                reads=[(yk, 0), (yk, 1), (yk, 2), 'gttab'], writes=[('ps', pb)])
                                    d_ = dst_[g % 2]
                                    dk = 'dst%d' % (g % 2)
                                    sch.op('act', lambda: nc.scalar.copy(out=d_[:, 0:2].rearrange("p k r c -> p (k r c)"), in_=ps[da][:, :]),
                                           reads=[('ps', da)], writes=[(dk, 0)])
                                    sch.op('dve', lambda: nc.vector.tensor_copy(out=d_[:, 2:4].rearrange("p k r c -> p (k r c)"), in_=ps[db][:, :]),
                                           reads=[('ps', db)], writes=[(dk, 1)])
                                    sch.dma('act', dd[g * 4:(g + 1) * 4, :].rearrange("k (r n c) -> n k r c", r=2, n=128),
                                            d_[:], reads=[(dk, 0), (dk, 1)])
                                return part_b
                            fft_fwd(ph, rows, NH, ep_conv)
                            sch.barrier()
                        with ExitStack() as ph:
                            Dl = kb.sb(ph, "Dl", [NK1, 2, 128, 128], BF16)
                            zc = kb.sb(ph, "zc", [128, L], BF16)
                            xg = kb.sb(ph, "xg", [128, L], BF16)
                            zo = kb.sb(ph, "zo", [128, L], BF16)
                            tf = [kb.sb(ph, "tf%d" % i, [128, 512], F32) for i in range(2)]
                            ddv = dd[:, :].rearrange("k (r n c) -> k r n c", r=2, n=128)
                            for hh in range(2):
                                for r in range(2):
                                    for p0 in range(0, NK1, 16):
                                        p1 = min(NK1, p0 + 16)
                                        sch.dma('sp', Dl[p0:p1, r, hh * 64:(hh + 1) * 64, :], ddv[p0:p1, r, hh * 64:(hh + 1) * 64, :],
                                                writes=[('Dl', r, hh, p0)])
                            sch.dma('sp', zc[:], rows, writes=['zc'])
                            grow = 512 + n * 512 + q * 128
                            sch.dma('sp', xg[:], uc[grow:grow + 128, :], writes=['xg'])
                            npb = min(128, 512 // NH)
                            zv = zc[:, :].rearrange("p (a b) -> p b a", b=128)
                            xv = xg[:, :].rearrange("p (a b) -> p b a", b=128)
                            ov = zo[:, :].rearrange("p (a b) -> p b a", b=128)
                            for gi, n1g in enumerate(range(0, 128, npb)):
                                pb = nextps()
                                for nl in range(npb):
                                    n1 = n1g + nl
                                    for r in range(2):
                                        sch.op('pe', lambda pb=pb, nl=nl, n1=n1, r=r: nc.tensor.matmul(
                                            out=ps[pb][:, nl * NH:(nl + 1) * NH], lhsT=Dl[:, r, n1, :], rhs=etab[:, r, :],
                                            start=(r == 0), stop=(r == 1)),
                                            reads=[('Dl', r, n1 // 64, p0) for p0 in range(0, NK1, 16)] + ['etab'], writes=[('ps', pb)])
                                t_ = tf[gi % 2]
                                tk = 'tf%d' % (gi % 2)
                                tv = t_[:, 0:npb * NH].rearrange("p (b a) -> p b a", a=NH)
                                pv = ps[pb][:, 0:npb * NH].rearrange("p (b a) -> p b a", a=NH)
                                sch.op('dve', lambda tv=tv, pv=pv, n1g=n1g: nc.vector.scalar_tensor_tensor(
                                    out=tv, in0=zv[:, n1g:n1g + npb, :], scalar=hsc[:, 1, n * 4 + q:n * 4 + q + 1],
                                    in1=pv, op0=ALU.mult, op1=ALU.add),
                                    reads=['zc', ('ps', pb), 'hsc'], writes=[tk])
                                tvT = t_[:, 0:npb * NH].rearrange("p (b a) -> p a b", a=NH)
                                xvT = xg[:, :].rearrange("p (a b) -> p a b", b=128)[:, :, n1g:n1g + npb]
                                ovT = zo[:, :].rearrange("p (a b) -> p a b", b=128)[:, :, n1g:n1g + npb]
                                sch.op('dve', lambda tvT=tvT, xvT=xvT, ovT=ovT: nc.vector.scalar_tensor_tensor(
                                    out=ovT, in0=tvT, scalar=hsc[:, 0, n * 4 + q:n * 4 + q + 1], in1=xvT, op0=ALU.mult, op1=ALU.mult),
                                    reads=[tk, 'xg', 'hsc'], writes=[('zo', gi)])
                            sch.dma('pool', zout[q * 128:(q + 1) * 128, :], zo[:], reads=[('zo', gi) for gi in range(128 // npb)])
                            sch.barrier()
                sch.barrier()


        def phase_qkrope(qkraw, qkraw_c, qr, kr, kcr):
            cos_d = kb.inp("rope_cos", [128, S])
            sin_d = kb.inp("rope_sin", [128, S])
            rmat_d = kb.inp("rope_R", [128, 128], BF16)
            bd_d = kb.inp("bd64", [128, 128], BF16)
            with ExitStack() as ph:
                cos_sb = kb.sb(ph, "cos", [128, S], F32)
                sin_sb = kb.sb(ph, "sin", [128, S], F32)
                rmat = kb.sb(ph, "rmat", [128, 128], BF16)
                bd = kb.sb(ph, "bd", [128, 128], BF16)
                raw = kb.sb(ph, "qraw", [128, S], BF16)
                sq = kb.sb(ph, "qsq", [128, S], BF16)
                rstd = kb.sb(ph, "qrstd", [128, S], F32)
                qn = kb.sb(ph, "qn", [128, S], BF16)
                ob = kb.sb(ph, "qob", [128, S], BF16)
                t1 = [kb.sb(ph, "qt1%d" % i, [128, NB], BF16) for i in range(2)]
                t2 = [kb.sb(ph, "qt2%d" % i, [128, NB], BF16) for i in range(2)]
                sch.dma('sp', cos_sb[:], cos_d, writes=['cos'])
                sch.dma('sp', sin_sb[:], sin_d, writes=['sin'])
                sch.dma('sp', rmat[:], rmat_d, writes=['rmat'])
                sch.dma('sp', bd[:], bd_d, writes=['bd'])
                tiles = [(qkraw[m * 128:(m + 1) * 128, :], S, 'qgain', True, qr[m * 128:(m + 1) * 128, :]) for m in range(8)]
                tiles += [(qkraw[1024 + P_ * 128:1024 + (P_ + 1) * 128, :], S, 'kgain', True, kr[P_ * 128:(P_ + 1) * 128, :]) for P_ in range(2)]
                tiles += [(qkraw_c[1024 + P_ * 128:1024 + (P_ + 1) * 128, :], C, 'kgain', False, kcr[P_ * 128:(P_ + 1) * 128, :]) for P_ in range(2)]
                cnt = 0
                for (srcr, T, gname, rope, dstr) in tiles:
                    nb = (T + NB - 1) // NB
                    sch.dma('sp', raw[:, :T], srcr, writes=['qraw'])
                    sch.op('act', lambda T=T: nc.scalar.activation(out=sq[:, :T], in_=raw[:, :T], func=AF.Square),
                           reads=['qraw'], writes=['qsq'])
                    rk = []
                    for b in range(nb):
                        n = min(NB, T - b * NB)
                        pb = nextps()
                        sch.op('pe', lambda pb=pb, b=b, n=n: nc.tensor.matmul(out=ps[pb][:, :n], lhsT=bd[:], rhs=sq[:, b * NB:b * NB + n],
                                                                         start=True, stop=True),
                               reads=['qsq', 'bd'], writes=[('ps', pb)])
                        sch.op('act', lambda pb=pb, b=b, n=n: nc.scalar.activation(out=rstd[:, b * NB:b * NB + n], in_=ps[pb][:, :n],
                                                                              func=AF.Ln, bias=epsc[:, 0:1], scale=1.0),
                               reads=[('ps', pb), 'epsc'], writes=[('qrstd', b)])
                        rk.append(('qrstd', b))
                    sch.op('act', lambda T=T: nc.scalar.activation(out=rstd[:, :T], in_=rstd[:, :T], func=AF.Exp, scale=-0.5),
                           reads=rk, writes=rk)
                    target = qn if rope else ob
                    tkeys = ['qn'] if rope else [('qob', b) for b in range(nb)]
                    sch.op('dve', lambda T=T, gname=gname, target=target: nc.vector.scalar_tensor_tensor(
                        out=target[:, :T], in0=raw[:, :T], scalar=smc(gname), in1=rstd[:, :T], op0=ALU.mult, op1=ALU.mult),
                        reads=rk + ['qraw', 'sm'], writes=tkeys)
                    if rope:
                        for b in range(nb):
                            cs = slice(b * NB, (b + 1) * NB)
                            pb = nextps()
                            sch.op('pe', lambda pb=pb, cs=cs: nc.tensor.matmul(out=ps[pb][:, :], lhsT=rmat[:], rhs=qn[:, cs], start=True, stop=True),
                                   reads=['qn', 'rmat'], writes=[('ps', pb)])
                            a1 = t1[cnt % 2]
                            a2 = t2[cnt % 2]
                            k1_ = 'qt1%d' % (cnt % 2)
                            k2_ = 'qt2%d' % (cnt % 2)
                            cnt += 1
                            sch.op('dve', lambda pb=pb, cs=cs, a1=a1: nc.vector.tensor_tensor(out=a1[:], in0=ps[pb][:, :], in1=sin_sb[:, cs], op=ALU.mult),
                                   reads=[('ps', pb), 'sin'], writes=[k1_])
                            sch.op('pool', lambda cs=cs, a2=a2: nc.gpsimd.tensor_tensor(out=a2[:], in0=qn[:, cs], in1=cos_sb[:, cs], op=ALU.mult),
                                   reads=['qn', 'cos'], writes=[k2_])
                            sch.op('dve', lambda cs=cs, a1=a1, a2=a2: nc.vector.tensor_tensor(out=ob[:, cs], in0=a1[:], in1=a2[:], op=ALU.add),
                                   reads=[k1_, k2_], writes=[('qob', b)])
                        sch.dma('act', dstr, ob[:, :T], reads=[('qob', b) for b in range(nb)])
                    else:
                        sch.dma('act', dstr, ob[:, :T], reads=[('qob', b) for b in range(nb)])
                sch.barrier()

        def phase_att(qr, kr, kcr, vtok, vctok, oT):
            mask_d = kb.inp("att_mask", [128, 2, 128], BF16)
            NQB = S // 128
            with ExitStack() as ph:
                kr_sb = kb.sb(ph, "kr_sb", [128, 2, S], BF16)
                kc_sb = kb.sb(ph, "kc_sb", [128, 2, C], BF16)
                vt = kb.sb(ph, "vt", [128, NQB, 260], BF16)
                vct = kb.sb(ph, "vct", [128, 2, 260], BF16)
                mask = kb.sb(ph, "mask", [128, 2, 128], BF16)
                ident = kb.sb(ph, "ident", [128, 128], BF16)
                esink = kb.sb(ph, "esink", [128, 16], F32)
                qb = [kb.sb(ph, "qb%d" % i, [128, 8, 128], BF16) for i in range(2)]
                Ptb = [kb.sb(ph, "Pt%d" % i, [128, 5, 512], BF16) for i in range(2)]
                obuf = kb.sb(ph, "obuf", [128, 16, 64], BF16)
                oTs = [kb.sb(ph, "oTs%d" % i, [128, 8, 128], BF16) for i in range(2)]
                den = kb.sb(ph, "den", [128, 4, 4], F32)
                for P_ in range(2):
                    sch.dma('sp', kr_sb[:, P_, :], kr[P_ * 128:(P_ + 1) * 128, :], writes=['kr_sb'])
                    sch.dma('sp', kc_sb[:, P_, :], kcr[P_ * 128:(P_ + 1) * 128, :], writes=['kc_sb'])
                vv = vtok.rearrange("(b p) w -> p b w", p=128)
                for b0 in range(0, NQB, 16):
                    sch.dma('sp', vt[:, b0:b0 + 16, :], vv[:, b0:b0 + 16, :], writes=['vt'])
                sch.dma('sp', vct[:], vctok.rearrange("(b p) w -> p b w", p=128), writes=['vct'])
                sch.dma('sp', mask[:], mask_d, writes=['mask'])
                sch.dma('sp', ident[:], ident_d, writes=['ident'])
                sch.op('act', lambda: nc.scalar.activation(out=esink[:], in_=smc('sink'), func=AF.Exp), reads=['sm'], writes=['esink'])
                sp_i = [0]

                def sbank():
                    b = sp_i[0]
                    sp_i[0] = (b + 1) % 4
                    return b
                for i in range(NQB):
                    q_ = qb[i % 2]
                    qk = 'qb%d' % (i % 2)
                    sch.dma('sp', q_[:], qr[:, i * 128:(i + 1) * 128].rearrange("(m p) t -> p m t", p=128), writes=[qk])
                    kbs = []
                    if i > 0:
                        kbs.append(('l', i - 1, 0))
                    kbs.append(('l', i, None))
                    if i < NQB - 1:
                        kbs.append(('l', i + 1, 1))
                    kbs += [('c', 0, None), ('c', 1, None)]
                    for g in range(4):
                        P_, half = g // 2, g % 2
                        rows = slice(half * 64, half * 64 + 64)
                        Pt = Ptb[g % 2]
                        ob_ = 4 + g
                        for idx, (kind, kb_, mi) in enumerate(kbs):
                            bank = sbank()
                            ksrc = kr_sb if kind == 'l' else kc_sb
                            kkey = 'kr_sb' if kind == 'l' else 'kc_sb'
                            sch.op('pe', lambda bank=bank, ksrc=ksrc, kb_=kb_, rows=rows, P_=P_: nc.tensor.matmul(
                                out=ps[bank][:, :], lhsT=ksrc[rows, P_, kb_ * 128:(kb_ + 1) * 128],
                                rhs=q_[rows, P_ * 4:(P_ + 1) * 4, :], start=True, stop=True),
                                reads=[kkey, qk], writes=[('ps', bank)])
                            sch.op('act', lambda bank=bank, idx=idx, Pt=Pt: nc.scalar.activation(
                                out=Pt[:, idx, :], in_=ps[bank][:, :], func=AF.Exp, scale=0.125),
                                reads=[('ps', bank)], writes=[('Pt', g % 2, idx)])
                            if mi is not None:
                                sch.op('dve', lambda idx=idx, mi=mi, Pt=Pt: nc.vector.tensor_tensor(
                                    out=Pt[:, idx, :].rearrange("p (j q) -> p j q", j=4),
                                    in0=Pt[:, idx, :].rearrange("p (j q) -> p j q", j=4),
                                    in1=mask[:, mi, :].unsqueeze(1).to_broadcast([128, 4, 128]), op=ALU.mult),
                                    reads=[('Pt', g % 2, idx), 'mask'], writes=[('Pt', g % 2, idx)])
                        for j in range(4):
                            for idx, (kind, kb_, mi) in enumerate(kbs):
                                vsrc = vt if kind == 'l' else vct
                                vkey = 'vt' if kind == 'l' else 'vct'
                                sch.op('pe', lambda j=j, idx=idx, vsrc=vsrc, kb_=kb_, ob_=ob_, Pt=Pt, g=g: nc.tensor.matmul(
                                    out=ps[ob_][:, j * 128:j * 128 + 65], lhsT=Pt[:, idx, j * 128:(j + 1) * 128],
                                    rhs=vsrc[:, kb_, g * 65:(g + 1) * 65], start=(idx == 0), stop=(idx == len(kbs) - 1)),
                                    reads=[('Pt', g % 2, idx), vkey], writes=[('ps', ob_)])
                        pv = ps[ob_][:, :].rearrange("p (j w) -> p j w", w=128)
                        sch.op('dve', lambda pv=pv, g=g: nc.vector.tensor_tensor(out=den[:, g, :], in0=pv[:, :, 64], in1=esink[:, g * 4:(g + 1) * 4], op=ALU.add),
                               reads=[('ps', ob_), 'esink'], writes=[('den', g)])
                        sch.op('dve', lambda g=g: nc.vector.reciprocal(out=den[:, g, :], in_=den[:, g, :]), reads=[('den', g)], writes=[('den', g)])
                        sch.op('dve', lambda pv=pv, g=g: nc.vector.tensor_tensor(
                            out=obuf[:, g * 4:(g + 1) * 4, :], in0=pv[:, :, 0:64],
                            in1=den[:, g, :].unsqueeze(2).to_broadcast([128, 4, 64]), op=ALU.mult),
                            reads=[('ps', ob_), ('den', g)], writes=[('obuf', g)])
                    bank = sbank()
                    pT = ps[bank][:, :].bitcast(BF16)
                    for m in range(8):
                        sch.op('pe', lambda m=m, pT=pT: nc.tensor.transpose(
                            out=pT[:, m * 128:(m + 1) * 128], in_=obuf[:, 2 * m:2 * m + 2, :].rearrange("p h d -> p (h d)"), identity=ident[:]),
                            reads=[('obuf', m // 2), 'ident'], writes=[('ps', bank)])
                    o_ = oTs[i % 2]
                    ok = 'oTs%d' % (i % 2)
                    sch.op('act', lambda pT=pT, o_=o_: nc.scalar.copy(out=o_[:].rearrange("p m q -> p (m q)"), in_=pT),
                           reads=[('ps', bank)], writes=[ok])
                    sch.dma('pool', oT[:, i * 128:(i + 1) * 128].rearrange("(m p) t -> p m t", p=128), o_[:], reads=[ok])
                sch.barrier()

        stg = stages if stages is not None else ALL_STAGES
        if 'mod' in stg:
            phase_mod()
        if 'l0x1' in stg:
            phase_x1(0, ab_w_in, 2560, xT, ctxT, px, pc, 0)
        if 'lru' in stg:
            phase_lru()
        if 'hyx' in stg:
            hyena_all(S, 'hx_', px, ymix)
        if 'hyc' in stg:
            hyena_all(C, 'hc_', pc, ymixc)
        if 'l0x2' in stg:
            phase_x2(0, ab_w_out, ymix, ymixc, xT, ctxT, xb0, ctxb0, True)
        if 'l1x1' in stg:
            def wload_qkv(w):
                for k in range(KC):
                    for P_ in range(2):
                        for hh in range(2):
                            sch.dma('pool', w[:, k, P_ * 512:(P_ + 1) * 512].rearrange("p (j h d) -> p j h d", j=4, h=2)[:, :, hh, :],
                                    at_w_qkv[k * 128:(k + 1) * 128, P_ * 512 + hh * 256:P_ * 512 + (hh + 1) * 256].rearrange("p (j d) -> p j d", j=4),
                                    writes=[('x1_w', k)])
                    sch.dma('pool', w[:, k, 1024:1280], at_w_qkv[k * 128:(k + 1) * 128, 1024:1280], writes=[('x1_w', k)])
            phase_x1(1, None, 1280, xb0, ctxb0, qkraw, qkraw_c, 0, gs=5, wload=wload_qkv, vproj=(at_w_qkv, vtok, vctok))
        if 'l1rope' in stg:
            phase_qkrope(qkraw, qkraw_c, qr, kr, kcr)
        if 'l1att' in stg:
            phase_att(qr, kr, kcr, vtok, vctok, oT)
        if 'l1x2' in stg:
            phase_x2(1, at_w_o, oT, None, xb0, None, outT, None, False)
        sch.barrier()
        kb.ninst = sch.ninst
    return kb


def make_in_map(inp, b, names, consts):
    m = {}
    for nme in names:
        if nme == 'xT':
            m[nme] = np.ascontiguousarray(np.asarray(inp['x'][b], np.float32).T)
        elif nme == 'ctxT':
            m[nme] = np.ascontiguousarray(np.asarray(inp['ctx'][b], np.float32).T)
        elif nme == 'smallp':
            m[nme] = build_small(inp, b)
        elif nme in consts:
            m[nme] = consts[nme]
        elif nme in ('ab_w_in', 'ab_w_out', 'at_w_qkv', 'at_w_o', 'hy_f_w1', 'hy_f_w2', 'hy_f_w3', 'lru_w_a', 'lru_w_i'):
            m[nme] = np.ascontiguousarray(np.asarray(inp[nme][0], np.float32))
        else:
            m[nme] = np.ascontiguousarray(np.asarray(inp[nme], np.float32))
    return m


ALL_STAGES = ['mod', 'l0x1', 'lru', 'hyx', 'hyc', 'l0x2', 'l1x1', 'l1rope', 'l1att', 'l1x2']


def kernel(**inputs):
    kb = build_program(dbg=(), stages=ALL_STAGES)
    consts = make_consts()
    names = list(kb.din.keys())
    in_maps = [make_in_map(inputs, b, names, consts) for b in range(8)]
    res = run_bass_kernel_spmd(kb.nc, in_maps, core_ids=list(range(8)))
    out = np.stack([np.ascontiguousarray(np.asarray(res.results[b]['outT'], np.float32).T) for b in range(8)], 0)
    return out
```
